# Optimizing a Trainium2 kernel written in Bass

```python
import math
import jax, jax.numpy as jnp
from jax import lax
import numpy as np

D_MODEL = 1024
BATCH = 4
SEQ = 4096
DEPTH = 2

CHUNK = 64
EPS = 1e-6
D_MIX = D_MODEL
CONV_K = 4

SSD_HEADS = 8
SSD_HEAD_DIM = 64
SSD_DIM = SSD_HEADS * SSD_HEAD_DIM
SSD_GROUPS = 2
SSD_STATE = 128
SSD_CONV_DIM = SSD_DIM + 2 * SSD_GROUPS * SSD_STATE

ATT_HEADS = 4
ATT_KV_HEADS = 2
ATT_HEAD_DIM = 64
ATT_DIM = ATT_HEADS * ATT_HEAD_DIM
ATT_KV_DIM = ATT_KV_HEADS * ATT_HEAD_DIM
WINDOW = 128
WIN_CHUNKS = WINDOW // CHUNK

GDN_HEADS = 4
GDN_HEAD_K = 64
GDN_HEAD_V = 64
GDN_KDIM = GDN_HEADS * GDN_HEAD_K
GDN_DIM = GDN_HEADS * GDN_HEAD_V
GDN_CONV_DIM = 2 * GDN_KDIM + GDN_DIM

FF = ((8 * D_MODEL // 3 + 255) // 256) * 256

IN_SIZES = (ATT_DIM, ATT_KV_DIM, ATT_KV_DIM,
            SSD_DIM, SSD_CONV_DIM, SSD_HEADS,
            GDN_CONV_DIM, GDN_DIM, GDN_HEADS, GDN_HEADS)
IN_COLS = sum(IN_SIZES)

kernel_name = "hybrid_ssd_swa_gdn_sandwich_block"


def _offsets(sizes):
    out, acc = [], 0
    for s in sizes[:-1]:
        acc += s
        out.append(acc)
    return out


def rmsnorm(x, w):
    xf = x.astype(jnp.float32)
    y = xf * lax.rsqrt(jnp.mean(xf * xf, axis=-1, keepdims=True) + EPS)
    return (y * w.astype(jnp.float32)).astype(x.dtype)


def l2norm(x):
    return x * lax.rsqrt(jnp.sum(x * x, axis=-1, keepdims=True) + EPS)


def causal_dwconv(x, w, b=None):
    k = w.shape[0]
    t = x.shape[1]
    xp = jnp.pad(x, ((0, 0), (k - 1, 0), (0, 0)))
    y = xp[:, 0:t] * w[0]
    for i in range(1, k):
        y = y + xp[:, i:i + t] * w[i]
    if b is not None:
        y = y + b
    return y


def alibi_slopes(n):
    return 2.0 ** (-8.0 * jnp.arange(1, n + 1, dtype=jnp.float32) / n)


def lower_exp_diff(cs):
    n = cs.shape[-1]
    tril = jnp.tril(jnp.ones((n, n), dtype=bool))
    diff = cs[..., :, None] - cs[..., None, :]
    return jnp.where(tril, jnp.exp(jnp.where(tril, diff, 0.0)), 0.0)


def swa_sink_alibi(q, k, v, sinks):
    bsz, t = q.shape[0], q.shape[1]
    nc = t // CHUNK
    grp = ATT_HEADS // ATT_KV_HEADS
    band = (WIN_CHUNKS + 1) * CHUNK
    qc = q.reshape(bsz, nc, CHUNK, ATT_KV_HEADS, grp, ATT_HEAD_DIM)
    padw = ((0, 0), (WIN_CHUNKS * CHUNK, 0), (0, 0))
    kp = jnp.pad(k, padw).reshape(bsz, nc + WIN_CHUNKS, CHUNK, ATT_KV_HEADS, ATT_HEAD_DIM)
    vp = jnp.pad(v, padw).reshape(bsz, nc + WIN_CHUNKS, CHUNK, ATT_KV_HEADS, ATT_HEAD_DIM)
    kb = jnp.concatenate([kp[:, j:j + nc] for j in range(WIN_CHUNKS + 1)], axis=2)
    vb = jnp.concatenate([vp[:, j:j + nc] for j in range(WIN_CHUNKS + 1)], axis=2)
    s = jnp.einsum('bcikgd,bcjkd->bckgij', qc, kb).astype(jnp.float32) * (ATT_HEAD_DIM ** -0.5)
    qi = jnp.arange(CHUNK)[:, None]
    kj = jnp.arange(band)[None, :]
    dist = jnp.abs(qi + WIN_CHUNKS * CHUNK - kj).astype(jnp.float32)
    slopes = alibi_slopes(ATT_HEADS).reshape(ATT_KV_HEADS, grp)
    s = s - slopes[:, :, None, None] * dist
    key_chunk = jnp.arange(nc)[:, None] - WIN_CHUNKS + (jnp.arange(band) // CHUNK)[None, :]
    valid = key_chunk >= 0
    s = jnp.where(valid[None, :, None, None, None, :], s, -jnp.inf)
    sink = jnp.broadcast_to(sinks.astype(jnp.float32).reshape(1, 1, ATT_KV_HEADS, grp, 1, 1),
                            s.shape[:-1] + (1,))
    p = jax.nn.softmax(jnp.concatenate([s, sink], axis=-1), axis=-1)[..., :band]
    o = jnp.einsum('bckgij,bcjkd->bcikgd', p.astype(v.dtype), vb)
    return o.reshape(bsz, t, ATT_DIM)


def ssd_mixer(z, xbc, dt, conv_w, conv_b, dt_bias, a_log, d_skip, norm_w):
    bsz, t = xbc.shape[0], xbc.shape[1]
    nc = t // CHUNK
    hpg = SSD_HEADS // SSD_GROUPS
    xbc = jax.nn.silu(causal_dwconv(xbc, conv_w, conv_b)).astype(jnp.float32)
    xs, bm, cm = jnp.split(xbc, [SSD_DIM, SSD_DIM + SSD_GROUPS * SSD_STATE], axis=-1)
    xs = xs.reshape(bsz, t, SSD_HEADS, SSD_HEAD_DIM)
    bm = jnp.repeat(bm.reshape(bsz, t, SSD_GROUPS, SSD_STATE), hpg, axis=2)
    cm = jnp.repeat(cm.reshape(bsz, t, SSD_GROUPS, SSD_STATE), hpg, axis=2)
    dt = jax.nn.softplus(dt.astype(jnp.float32) + dt_bias.astype(jnp.float32))
    a = -jnp.exp(a_log.astype(jnp.float32))
    xc = (xs * dt[..., None]).reshape(bsz, nc, CHUNK, SSD_HEADS, SSD_HEAD_DIM)
    bc = bm.reshape(bsz, nc, CHUNK, SSD_HEADS, SSD_STATE)
    cc = cm.reshape(bsz, nc, CHUNK, SSD_HEADS, SSD_STATE)
    da = (dt * a).reshape(bsz, nc, CHUNK, SSD_HEADS).transpose(0, 3, 1, 2)
    a_cs = jnp.cumsum(da, axis=-1)
    lmat = lower_exp_diff(a_cs)
    scores = jnp.einsum('bclhn,bcshn->bhcls', cc, bc) * lmat
    y_diag = jnp.einsum('bhcls,bcshp->bclhp', scores, xc)
    decay_states = jnp.exp(a_cs[..., -1:] - a_cs)
    chunk_states = jnp.einsum('bclhn,bhcl,bclhp->bchpn', bc, decay_states, xc)
    chunk_decay = jnp.exp(a_cs[..., -1])

    def step(state, inp):
        new, dec = inp
        return state * dec[..., None, None] + new, state

    init = jnp.zeros((bsz, SSD_HEADS, SSD_HEAD_DIM, SSD_STATE), jnp.float32)
    _, prev = lax.scan(step, init, (chunk_states.transpose(1, 0, 2, 3, 4),
                                    chunk_decay.transpose(2, 0, 1)))
    prev = prev.transpose(1, 0, 2, 3, 4)
    y_off = jnp.einsum('bclhn,bchpn,bhcl->bclhp', cc, prev, jnp.exp(a_cs))
    y = (y_diag + y_off).reshape(bsz, t, SSD_HEADS, SSD_HEAD_DIM) + xs * d_skip.astype(jnp.float32)[:, None]
    g = y.reshape(bsz, t, SSD_DIM) * jax.nn.silu(z.astype(jnp.float32))
    g = g.reshape(bsz, t, SSD_GROUPS, SSD_DIM // SSD_GROUPS)
    g = g * lax.rsqrt(jnp.mean(g * g, axis=-1, keepdims=True) + EPS)
    return (g.reshape(bsz, t, SSD_DIM) * norm_w.astype(jnp.float32)).astype(z.dtype)


def gdn_mixer(qkv, z, b, a, conv_w, dt_bias, a_log, norm_w):
    bsz, t = qkv.shape[0], qkv.shape[1]
    nc = t // CHUNK
    qkv = jax.nn.silu(causal_dwconv(qkv, conv_w)).astype(jnp.float32)
    q, k, v = jnp.split(qkv, [GDN_KDIM, 2 * GDN_KDIM], axis=-1)
    q = l2norm(q.reshape(bsz, t, GDN_HEADS, GDN_HEAD_K)) * (GDN_HEAD_K ** -0.5)
    k = l2norm(k.reshape(bsz, t, GDN_HEADS, GDN_HEAD_K))
    v = v.reshape(bsz, t, GDN_HEADS, GDN_HEAD_V)
    beta = jax.nn.sigmoid(b.astype(jnp.float32))
    g = -jnp.exp(a_log.astype(jnp.float32)) * jax.nn.softplus(a.astype(jnp.float32) + dt_bias.astype(jnp.float32))

    def chunks(u):
        return u.reshape(bsz, nc, CHUNK, GDN_HEADS, u.shape[-1]).transpose(0, 3, 1, 2, 4)

    qc, kc, vc = chunks(q), chunks(k), chunks(v)
    betac = beta.reshape(bsz, nc, CHUNK, GDN_HEADS).transpose(0, 3, 1, 2)
    gc = jnp.cumsum(g.reshape(bsz, nc, CHUNK, GDN_HEADS).transpose(0, 3, 1, 2), axis=-1)
    decay = lower_exp_diff(gc)
    kbeta = kc * betac[..., None]
    strict = jnp.tril(jnp.einsum('bhcid,bhcjd->bhcij', kbeta, kc) * decay, -1)
    rhs = jnp.concatenate([vc * betac[..., None], kbeta * jnp.exp(gc)[..., None]], axis=-1)
    sol = lax.linalg.triangular_solve(strict, rhs, left_side=True, lower=True, unit_diagonal=True)
    u, w = jnp.split(sol, [GDN_HEAD_V], axis=-1)
    qk = jnp.einsum('bhcid,bhcjd->bhcij', qc, kc) * decay

    def step(state, inp):
        q_i, k_i, u_i, w_i, qk_i, g_i = inp
        v_new = u_i - jnp.einsum('bhld,bhde->bhle', w_i, state)
        o = (jnp.einsum('bhld,bhde->bhle', q_i * jnp.exp(g_i)[..., None], state)
             + jnp.einsum('bhls,bhse->bhle', qk_i, v_new))
        g_last = g_i[..., -1]
        state = (state * jnp.exp(g_last)[..., None, None]
                 + jnp.einsum('bhld,bhle->bhde', k_i * jnp.exp(g_last[..., None] - g_i)[..., None], v_new))
        return state, o

    xs = tuple(jnp.moveaxis(u_, 2, 0) for u_ in (qc, kc, u, w, qk, gc))
    init = jnp.zeros((bsz, GDN_HEADS, GDN_HEAD_K, GDN_HEAD_V), jnp.float32)
    _, o = lax.scan(step, init, xs)
    o = o.transpose(1, 0, 3, 2, 4).reshape(bsz, t, GDN_HEADS, GDN_HEAD_V)
    o = o * lax.rsqrt(jnp.mean(o * o, axis=-1, keepdims=True) + EPS) * norm_w.astype(jnp.float32)
    o = o * jax.nn.silu(z.astype(jnp.float32).reshape(bsz, t, GDN_HEADS, GDN_HEAD_V))
    return o.reshape(bsz, t, GDN_DIM).astype(z.dtype)


def setup_inputs(seed: int = 0) -> dict:
    key = jax.random.key(seed)
    ks = jax.random.split(key, 24)
    L = DEPTH

    def nrm(k, shape, scale):
        return jax.random.normal(k, shape, jnp.float32) * scale

    def gain(k, shape):
        return 1.0 + 0.05 * jax.random.normal(k, shape, jnp.float32)

    def dt_bias_init(k, shape):
        u = jax.random.uniform(k, shape, jnp.float32, math.log(1e-3), math.log(1e-1))
        dtv = jnp.exp(u)
        return dtv + jnp.log(-jnp.expm1(-dtv))

    def a_log_init(k, shape):
        return jnp.log(jax.random.uniform(k, shape, jnp.float32, 1.0, 16.0))

    return {
        "x": nrm(ks[0], (BATCH, SEQ, D_MODEL), 1.0),
        "pre_mix_norm": gain(ks[1], (L, D_MODEL)),
        "post_mix_norm": gain(ks[2], (L, D_MODEL)),
        "pre_ffn_norm": gain(ks[3], (L, D_MODEL)),
        "post_ffn_norm": gain(ks[4], (L, D_MODEL)),
        "w_in": nrm(ks[5], (L, D_MODEL, IN_COLS), D_MODEL ** -0.5),
        "w_out": nrm(ks[6], (L, D_MIX, D_MODEL), D_MIX ** -0.5),
        "attn_sinks": nrm(ks[7], (L, ATT_HEADS), 0.5),
        "ssd_conv_w": nrm(ks[8], (L, CONV_K, SSD_CONV_DIM), CONV_K ** -0.5),
        "ssd_conv_b": nrm(ks[9], (L, SSD_CONV_DIM), 0.01),
        "ssd_dt_bias": dt_bias_init(ks[10], (L, SSD_HEADS)),
        "ssd_A_log": a_log_init(ks[11], (L, SSD_HEADS)),
        "ssd_D": 1.0 + 0.1 * jax.random.normal(ks[12], (L, SSD_HEADS), jnp.float32),
        "ssd_norm_w": gain(ks[13], (L, SSD_DIM)),
        "gdn_conv_w": nrm(ks[14], (L, CONV_K, GDN_CONV_DIM), CONV_K ** -0.5),
        "gdn_dt_bias": dt_bias_init(ks[15], (L, GDN_HEADS)),
        "gdn_A_log": a_log_init(ks[16], (L, GDN_HEADS)),
        "gdn_norm_w": gain(ks[17], (L, GDN_HEAD_V)),
        "ffn_w_gate": nrm(ks[18], (L, D_MODEL, FF), D_MODEL ** -0.5),
        "ffn_w_up": nrm(ks[19], (L, D_MODEL, FF), D_MODEL ** -0.5),
        "ffn_w_down": nrm(ks[20], (L, FF, D_MODEL), FF ** -0.5),
    }


def reference(x, pre_mix_norm, post_mix_norm, pre_ffn_norm, post_ffn_norm, w_in, w_out,
              attn_sinks, ssd_conv_w, ssd_conv_b, ssd_dt_bias, ssd_A_log, ssd_D, ssd_norm_w,
              gdn_conv_w, gdn_dt_bias, gdn_A_log, gdn_norm_w, ffn_w_gate, ffn_w_up, ffn_w_down):
    offs = _offsets(IN_SIZES)
    for l in range(DEPTH):
        h = rmsnorm(x, pre_mix_norm[l])
        proj = h @ w_in[l]
        (a_q, a_k, a_v, s_z, s_xbc, s_dt, g_qkv, g_z, g_b, g_a) = jnp.split(proj, offs, axis=-1)
        att = swa_sink_alibi(a_q, a_k, a_v, attn_sinks[l])
        ssd = ssd_mixer(s_z, s_xbc, s_dt, ssd_conv_w[l], ssd_conv_b[l], ssd_dt_bias[l],
                        ssd_A_log[l], ssd_D[l], ssd_norm_w[l])
        gdn = gdn_mixer(g_qkv, g_z, g_b, g_a, gdn_conv_w[l], gdn_dt_bias[l], gdn_A_log[l], gdn_norm_w[l])
        mix = jnp.concatenate([att, ssd, gdn], axis=-1) @ w_out[l]
        x = x + rmsnorm(mix, post_mix_norm[l])
        h = rmsnorm(x, pre_ffn_norm[l])
        f = (jax.nn.silu(h @ ffn_w_gate[l]) * (h @ ffn_w_up[l])) @ ffn_w_down[l]
        x = x + rmsnorm(f, post_ffn_norm[l])
    return x
```

```python
import numpy as np
from contextlib import ExitStack
import concourse.bass as bass
import concourse.mybir as mybir
from concourse.ap import AP
from concourse.bass_utils import run_bass_kernel_spmd

F32 = mybir.dt.float32
BF16 = mybir.dt.bfloat16
AF = mybir.ActivationFunctionType
ALU = mybir.AluOpType
AX = mybir.AxisListType

EPOCH = 2048
D = 1024
FF = 2816
NFB = FF // 128
EPS = 1e-6


class Emit:
    ENG = ("pe", "act", "dve", "pool", "sp")

    def __init__(self, nc, stack):
        self.nc = nc
        self.stack = stack
        self.lists = {e: [] for e in self.ENG}
        self.cnt = {e: 0 for e in self.ENG}
        self.esems = {e: [] for e in self.ENG}
        self.dsems = {}
        self.waited = {e: {} for e in self.ENG}
        self.lastw = {}
        self.reads = {}
        self.ninst = 0

    def _newsem(self, name):
        return self.stack.enter_context(self.nc.semaphore(name))

    def _esem(self, eng, idx):
        lst = self.esems[eng]
        while len(lst) <= idx:
            lst.append(self._newsem(f"e_{eng}_{len(lst)}"))
        return lst[idx]

    def _engobj(self, eng):
        nc = self.nc
        return {"pe": nc.tensor, "act": nc.scalar, "dve": nc.vector, "pool": nc.gpsimd, "sp": nc.sync}[eng]

    def _deps(self, eng, reads, writes):
        deps = {}

        def add(ev, same_ok):
            semname, sem, val, weng = ev
            if weng == eng and same_ok:
                return
            if deps.get(semname, (None, 0))[1] < val:
                deps[semname] = (sem, val)

        for k in reads:
            ev = self.lastw.get(k)
            if ev is not None:
                add(ev, same_ok=(eng == "pe"))
        for k in writes:
            ev = self.lastw.get(k)
            if ev is not None:
                add(ev, same_ok=(eng == "pe"))
            for ev in self.reads.get(k, {}).values():
                add(ev, same_ok=(eng == "pe"))
        out = []
        for semname, (sem, val) in deps.items():
            if self.waited[eng].get(semname, 0) < val:
                self.waited[eng][semname] = val
                out.append((sem, val))
        return out

    def _record(self, ev, reads, writes):
        for k in writes:
            self.lastw[k] = ev
            self.reads[k] = {}
        for k in reads:
            self.reads.setdefault(k, {})[(ev[0], ev[3])] = ev

    def ops(self, eng, fns, reads=(), writes=(), clobbers=(), after_self=False):
        reads = list(reads)
        writes = list(writes)
        excl = [k for k in reads if k[0] == "B" and k[1:].isdigit()]
        waits = self._deps(eng, reads, writes + list(clobbers) + excl)
        if after_self and self.cnt[eng] > 0:
            pidx = (self.cnt[eng] - 1) // EPOCH
            pname = f"e_{eng}_{pidx}"
            pval = self.cnt[eng] - pidx * EPOCH
            if self.waited[eng].get(pname, 0) < pval:
                self.waited[eng][pname] = pval
                waits.append((self.esems[eng][pidx], pval))
        idx = self.cnt[eng] // EPOCH
        sem = self._esem(eng, idx)
        self.cnt[eng] += 1
        val = self.cnt[eng] - idx * EPOCH
        semname = f"e_{eng}_{idx}"
        e = self._engobj(eng)

        def run(waits=waits, fns=fns, sem=sem, e=e):
            for s, v in waits:
                e.wait_ge(s, v)
            for f in fns[:-1]:
                f()
            fns[-1]().then_inc(sem, 1)

        self.lists[eng].append(run)
        self.ninst += len(fns)
        self._record((semname, sem, val, eng), reads, writes)

    def op(self, eng, fn, reads=(), writes=(), clobbers=()):
        self.ops(eng, [fn], reads, writes, clobbers)

    def dma(self, semname, pairs, reads=(), writes=(), eng="sp", clobbers=()):
        reads = list(reads)
        writes = list(writes)
        waits = self._deps(eng, reads, writes + list(clobbers))
        if semname not in self.dsems:
            self.dsems[semname] = [self._newsem("d_" + semname), 0]
        ent = self.dsems[semname]
        sem = ent[0]
        ent[1] += 16 * len(pairs)
        val = ent[1]
        e = self._engobj(eng)

        def run(waits=waits, pairs=pairs, sem=sem, e=e):
            for s, v in waits:
                e.wait_ge(s, v)
            for o, i in pairs:
                e.dma_start(out=o, in_=i).then_inc(sem, 16)

        self.lists[eng].append(run)
        self.ninst += len(pairs)
        self._record(("d_" + semname, sem, val, "dma"), reads, writes)

    def barrier(self):
        targets = []
        for e in self.ENG:
            if self.cnt[e] > 0:
                idx = (self.cnt[e] - 1) // EPOCH
                targets.append((f"e_{e}_{idx}", self.esems[e][idx], self.cnt[e] - idx * EPOCH, e))
        for name, (sem, val) in self.dsems.items():
            if val > 0:
                targets.append(("d_" + name, sem, val, "dma"))
        for eng in self.ENG:
            waits = []
            for semname, sem, val, weng in targets:
                if weng == eng:
                    continue
                if self.waited[eng].get(semname, 0) < val:
                    self.waited[eng][semname] = val
                    waits.append((sem, val))
            e = self._engobj(eng)

            def run(waits=waits, e=e):
                for s_, v in waits:
                    e.wait_ge(s_, v)

            self.lists[eng].append(run)

    def finish(self, final_keys):
        waits = self._deps("sp", [], list(final_keys))
        e = self._engobj("sp")

        def run():
            for s, v in waits:
                e.wait_ge(s, v)

        self.lists["sp"].append(run)

    def emit(self):
        nc = self.nc
        with nc.Block() as block:
            @block.sync
            def _(eng):
                for f in self.lists["sp"]:
                    f()

            @block.tensor
            def _(eng):
                for f in self.lists["pe"]:
                    f()

            @block.scalar
            def _(eng):
                for f in self.lists["act"]:
                    f()

            @block.vector
            def _(eng):
                for f in self.lists["dve"]:
                    f()

            @block.gpsimd
            def _(eng):
                for f in self.lists["pool"]:
                    f()


def bc(ap, dims):
    src = list(ap.ap)
    out = [list(src[0])]
    it = iter(src[1:])
    for d in dims:
        if d == "k":
            s = next(it)
            out.append([s[0], s[1]])
        else:
            out.append([0, d[1]])
    return AP(ap.tensor, ap.offset, out)


def _win_perm():
    segs = [(0, 64), (128, 64), (64, 64), (192, 64), (256, 128), (1024, 1024), (2056, 768),
            (512, 512), (384, 128), (2824, 256), (2048, 8), (3080, 8)]
    return np.concatenate([np.arange(s, s + n) for s, n in segs])


C_TM = 17 * 128
C_Z, C_AV, C_GZ, C_SM = C_TM, C_TM + 512, C_TM + 640, C_TM + 896


class _Stop(Exception):
    pass


def build_program(NT, L, dbg_stop=None):
    T = NT * 128
    FT = 1
    NFT = NT // FT
    nc = bass.Bass("TRN2", target_bir_lowering=False)
    st = ExitStack()
    em = Emit(nc, st)

    def din(name, shape):
        return nc.dram_tensor(name, list(shape), F32, kind="ExternalInput").ap()

    x_d = din("x", [T, D])
    n_pm, n_po, n_pf, n_pof = (din(n, [L, D]) for n in ("pre_mix_norm", "post_mix_norm", "pre_ffn_norm", "post_ffn_norm"))
    w_in_d = din("w_in", [L, D, 3088])
    w_out_d = din("w_out", [L, D, D])
    sinks_d = din("attn_sinks", [L, 4])
    scw_d = din("ssd_conv_w", [L, 128, 8, 4])
    scb_d = din("ssd_conv_b", [L, 1024])
    sdtb_d = din("ssd_dt_bias", [L, 8])
    sal_d = din("ssd_A_log", [L, 8])
    sD_d = din("ssd_D", [L, 8])
    snw_d = din("ssd_norm_w", [L, 512])
    gcw_d = din("gdn_conv_w", [L, 128, 6, 4])
    gdtb_d = din("gdn_dt_bias", [L, 4])
    gal_d = din("gdn_A_log", [L, 4])
    gnw_d = din("gdn_norm_w", [L, 64])
    wg_d = din("ffn_w_gate", [L, D, FF])
    wu_d = din("ffn_w_up", [L, D, FF])
    wd_d = din("ffn_w_down", [L, FF, D])
    c_id = din("c_ident", [128, 128])
    c_U = din("c_U", [128, 128])
    c_SL = din("c_SL", [128, 128])
    c_US = din("c_US", [128, 128])
    c_dm = din("c_dmat", [128, 1024])
    out_d = nc.dram_tensor("out", [T, D], F32, kind="ExternalOutput").ap()
    dbg_d = nc.dram_tensor("dbg", [128, D], F32, kind="ExternalOutput").ap() if dbg_stop == "dbgcat" else None
    xa_d = nc.dram_tensor("xa_scr", [T, D], F32).ap()
    xb_d = nc.dram_tensor("xb_scr", [T, D], F32).ap()

    def sb(name, shape, dt=F32):
        return st.enter_context(nc.sbuf_tensor(name, list(shape), dt))[:]

    B0 = st.enter_context(nc.psum_tensor("B0", [128, 1024], BF16))[:]
    BK = [None] + [st.enter_context(nc.psum_tensor(f"B{i}", [128, 512], F32))[:] for i in range(1, 8)]

    ACOLS = 3 * 8 * FF
    arena = sb("arena", [128, ACOLS], BF16)
    w_in_sb = arena[:, 0:8 * 3088].rearrange("p (k c) -> p k c", k=8)
    w_out_sb = arena[:, 8 * 3088:8 * 3088 + 8 * 1024].rearrange("p (k c) -> p k c", k=8)
    wg_sb = arena[:, 0:8 * FF].rearrange("p (k c) -> p k c", k=8)
    wu_sb = arena[:, 8 * FF:16 * FF].rearrange("p (k c) -> p k c", k=8)
    wd_sb = arena[:, 16 * FF:16 * FF + NFB * 1024].rearrange("p (k c) -> p k c", k=NFB)
    WM_KEYS = [f"win{k}" for k in range(8)] + [f"wout{k}" for k in range(8)]
    WF_KEYS = [f"wg{k}" for k in range(8)] + [f"wu{k}" for k in range(8)] + [f"wd{k}" for k in range(NFB)]
    tail = [8 * 3088 + 8 * 1024]
    AUXC = NFB * 128 + 2 * 1024 + 1024 + 2 * 2048
    aux = sb("aux", [128, AUXC], BF16)
    auxm = [0]
    auxf = [0]

    def carve(region, cur, limit, shape, dt):
        n = int(np.prod(shape[1:]))
        cols = n * (2 if dt == F32 else 1)
        cols += cols % 2
        if cur[0] + cols > limit:
            return None
        v = region[:, cur[0]:cur[0] + cols]
        cur[0] += cols
        v = v.bitcast(F32) if dt == F32 else v[:, 0:n]
        if len(shape) == 3:
            v = v.rearrange("p (a b) -> p a b", a=shape[1])
        if shape[0] != 128:
            v = v[0:shape[0]]
        return v

    def tb(name, shape, dt=F32):
        v = carve(arena, tail, ACOLS, shape, dt)
        if v is None:
            v = carve(aux, auxm, AUXC, shape, dt)
        return v if v is not None else sb(name, shape, dt)

    def fb(name, shape, dt=F32):
        v = carve(aux, auxf, AUXC, shape, dt)
        return v if v is not None else sb(name, shape, dt)

    ident_f = sb("ident_f", [128, 128]); ident_b = sb("ident_b", [128, 128], BF16)
    U_f = sb("U_f", [128, 128]); SL_f = sb("SL_f", [128, 128]); US_f = sb("US_f", [128, 128])
    ones_f = sb("ones_f", [128, 128]); ones_b = sb("ones_b", [1, 128], BF16)
    em.dma("c0", [(ident_f[:], c_id), (U_f[:], c_U), (SL_f[:], c_SL), (US_f[:], c_US)],
           writes=["ident_f", "U_f", "SL_f", "US_f"])
    em.op("dve", lambda: nc.vector.tensor_copy(ident_b[:], ident_f[:]), ["ident_f"], ["ident_b"])
    em.op("pool", lambda: nc.gpsimd.memset(ones_f[:], 1.0), [], ["ones_f"])
    em.op("pool", lambda: nc.gpsimd.memset(ones_b[:], 1.0), [], ["ones_b"])

    g_pf = fb("g_pf", [128, D]); g_pof = fb("g_pof", [128, D])
    prm = sb("prm", [128, 40])
    drv = sb("drv", [128, 16])
    cw = sb("cw", [128, 14, 4])
    xt = [sb(f"xt{i}", [128, D]) for i in range(3)]
    xo = sb("xo", [128, D])
    sqj = sb("sqj", [128, D], BF16)
    hb = sb("hb", [128, D], BF16)
    hT2 = [sb("hT0", [128, 8, 128], BF16), fb("hT1", [128, 8, 128], BF16)]
    hT = hT2[0]
    stat = sb("stat", [128, 16])
    etmp = [sb(f"etmp{i}", [128, 512]) for i in range(2)]
    hidT = fb("hidT", [128, NFB, 128], BF16)
    sgt2 = [fb(f"sgt{i}", [128, 512]) for i in range(2)]
    att4 = sb("att4", [128, 8])
    s8 = sb("s8", [128, 64])
    g8 = sb("g8", [128, 96])
    sK = sb("sK", [128, 4, 4]); sQ = sb("sQ", [128, 2, 4])
    sm16 = [sb(f"sm16_{i}", [128, 16]) for i in range(2)]
    dW = tb("dW", [128, 56, 128], BF16)
    g_pm = tb("g_pm", [128, D]); g_po = tb("g_po", [128, D])
    dmat = tb("dmat", [128, 1024])
    nwS = tb("nwS", [128, 512]); nwG = tb("nwG", [128, 64])
    cbrow = tb("cbrow", [1, 1024], BF16)
    xfm = tb("xfm", [128, 14, 132], BF16)
    cvs = [tb(f"cvs{i}", [128, 14, 128], BF16) for i in range(2)]
    qT = [tb(f"qT{i}", [128, 2, 128], BF16) for i in range(2)]
    kT = [tb(f"kT{i}", [128, 128], BF16) for i in range(3)]
    vaug = [tb(f"vaug{i}", [128, 2, 66], BF16) for i in range(3)]
    zs = [tb(f"zs{i}", [128, 512]) for i in range(2)]
    zsg = [tb(f"zsg{i}", [128, 256]) for i in range(2)]
    tm1 = [tb(f"tm1_{i}", [128, 768], BF16) for i in range(2)]
    tm2 = [tb(f"tm2_{i}", [128, 768], BF16) for i in range(2)]
    ex = tb("ex", [128, 1024]); pT = tb("pT", [128, 1024], BF16)
    cat = tb("cat", [128, D], BF16); catT = tb("catT", [128, 8, 128], BF16)
    lhsTd = tb("lhsTd", [128, 8, 128]); LT = tb("LT", [128, 8, 128], BF16); Gm = tb("Gm", [128, 2, 128], BF16)
    scT = tb("scT", [128, 8, 128], BF16)
    xc = tb("xc", [128, 512], BF16); xdec = tb("xdec", [128, 512], BF16)
    y1 = tb("y1", [128, 512]); y2 = tb("y2", [128, 512])
    S = tb("S", [128, 512]); Sbf = tb("Sbf", [128, 512], BF16)
    sqk = tb("sqk", [128, 512])
    Kv = tb("Kv", [128, 4, 256], BF16); Qv = tb("Qv", [128, 2, 256], BF16); vb = tb("vb", [128, 256], BF16)
    FMg = tb("FMg", [128, 8, 128], BF16)
    lDD = tb("lDD", [128, 8, 128])
    lD = lDD[:, 0:4, :]; lDT = lDD[:, 4:8, :]
    eD = tb("eD", [128, 512]); eDT = tb("eDT", [128, 512]); eDT2 = tb("eDT2", [128, 512])
    XmF = tb("XmF", [128, 4, 128]); YmF = tb("YmF", [128, 4, 128])
    TTF = tb("TTF", [128, 4, 128])
    TT = XmF.rearrange("p h c -> p (h c)").bitcast(BF16)[:, 0:512].rearrange("p (h c) -> p h c", h=4)
    QKm = tb("QKm", [128, 4, 128], BF16)
    negwT = tb("negwT", [128, 2, 128], BF16)
    vnew = tb("vnew", [128, 256], BF16)
    Sg = tb("Sg", [128, 2, 64]); Sgb = tb("Sgb", [128, 2, 64], BF16)
    osb = tb("osb", [128, 256]); sqo = tb("sqo", [128, 256])

    V = {"dve": nc.vector, "pool": nc.gpsimd}

    def tt(eng, out, a, b, op, r, w):
        em.op(eng, lambda: V[eng].tensor_tensor(out, a, b, op), r, w)

    def ts(eng, out, a, s1, s2, op0, op1, r, w):
        if s2 is None:
            em.op(eng, lambda: V[eng].tensor_scalar(out, a, s1, None, op0), r, w)
        else:
            em.op(eng, lambda: V[eng].tensor_scalar(out, a, s1, s2, op0, op1), r, w)

    def stt(eng, out, a, s, b, op0, op1, r, w):
        em.op(eng, lambda: V[eng].scalar_tensor_tensor(out, a, s, b, op0, op1), r, w)

    def cp(eng, out, a, r, w):
        if eng == "act":
            em.op("act", lambda: nc.scalar.copy(out, a), r, w)
        else:
            em.op(eng, lambda: V[eng].tensor_copy(out, a), r, w)

    def act(out, a, func, r, w, bias=None, scale=None, accum=None):
        kw = {}
        if bias is not None:
            kw["bias"] = bias
        if scale is not None:
            kw["scale"] = scale
        if accum is not None:
            kw["accum_out"] = accum
        em.op("act", lambda: nc.scalar.activation(out, a, func, **kw), r, w)

    chk_cnt = {}

    def chk(label):
        chk_cnt[label] = chk_cnt.get(label, 0) + 1
        if dbg_stop == label or dbg_stop == f"{label}#{chk_cnt[label]}":
            raise _Stop()

    def mm(out, lhsT, rhs, start=True, stop=True):
        return lambda: nc.tensor.matmul(out, lhsT, rhs, start=start, stop=stop)

    def tr(out, a):
        return lambda: nc.tensor.transpose(out, a, ident_b[:])

    def rsqrt_cols(ap, scale, r, w):
        act(ap, ap, AF.Ln, r, w, bias=EPS, scale=scale)
        act(ap, ap, AF.Exp, w, w, scale=-0.5)

    def silu(src, out, tmp, tmpk, r, w):
        act(tmp, src, AF.Exp, r, [tmpk], scale=-1.0)
        ts("pool", tmp, tmp, 1.0, None, ALU.add, None, [tmpk], [tmpk])
        em.op("dve", lambda: nc.vector.reciprocal(tmp, tmp), [tmpk], [tmpk])
        tt("dve", out, src, tmp, ALU.mult, r + [tmpk], w)

    def rmsnorm_to_hT(xtile, xkey, gain, gkey, ncols_off, hTt, hkey):
        act(sqj[:], xtile[:], AF.Square, [xkey], ["sqj", "stat0"], accum=stat[:, 0:1])
        rsqrt_cols(stat[:, 0:1], 1.0 / D, ["stat0"], ["stat0"])
        stt("dve", hb[:], xtile[:], stat[:, 0:1], gain[:], ALU.mult, ALU.mult, [xkey, "stat0", gkey], ["hb"])
        em.ops("pe", [tr(B0[:, k * 128:(k + 1) * 128], hb[:, k * 128:(k + 1) * 128]) for k in range(8)],
               ["hb", "ident_b"], ["B0"])
        cp("act", hTt[:, :, ncols_off:ncols_off + 128], B0[:].rearrange("p (k c) -> p k c", k=8), ["B0"], [hkey])

    def post_norm_residual(banks, bkeys, xin, xkey, gain, gkey):
        for i, (b, bk) in enumerate(zip(banks, bkeys)):
            act(sqj[:, 0:512], b[:, 0:512], AF.Square, [bk], ["sqj", f"stat{1 + i}"], accum=stat[:, 1 + i:2 + i])
        tt("dve", stat[:, 3:4], stat[:, 1:2], stat[:, 2:3], ALU.add, ["stat1", "stat2"], ["stat3"])
        rsqrt_cols(stat[:, 3:4], 1.0 / D, ["stat3"], ["stat3"])
        for i, (b, bk) in enumerate(zip(banks, bkeys)):
            stt("dve", xo[:, i * 512:(i + 1) * 512], b[:, 0:512], stat[:, 3:4], gain[:, i * 512:(i + 1) * 512],
                ALU.mult, ALU.mult, [bk, "stat3", gkey], [f"xo{i}"])
            tt("pool", xin[:, i * 512:(i + 1) * 512], xo[:, i * 512:(i + 1) * 512], xin[:, i * 512:(i + 1) * 512],
               ALU.add, [f"xo{i}", xkey], [xkey])

    def load_weights_M(l):
        pairs = [(w_in_sb[:, k, :], w_in_d[l, k * 128:(k + 1) * 128, :]) for k in range(8)]
        pairs += [(w_out_sb[:, k, :], w_out_d[l, k * 128:(k + 1) * 128, :]) for k in range(8)]
        em.dma("wM", pairs, writes=WM_KEYS, eng="pool")

    def load_weights_F(l):
        pairs = []
        for k in range(8):
            pairs.append((wg_sb[:, k, :], wg_d[l, k * 128:(k + 1) * 128, :]))
            pairs.append((wu_sb[:, k, :], wu_d[l, k * 128:(k + 1) * 128, :]))
        for k in range(NFB):
            pairs.append((wd_sb[:, k, :], wd_d[l, k * 128:(k + 1) * 128, :]))
        em.dma("wF", pairs, writes=WF_KEYS, eng="pool")
        em.dma("prmF", [(g_pf[:], n_pf[l, :].partition_broadcast(128)), (g_pof[:], n_pof[l, :].partition_broadcast(128))],
               writes=["g_pf", "g_pof"])

    def load_params(l):
        em.dma("prm", [(g_pm[:], n_pm[l, :].partition_broadcast(128)), (g_po[:], n_po[l, :].partition_broadcast(128)),
                       (cw[:, 0:8, :], scw_d[l]), (cw[:, 8:14, :], gcw_d[l]),
                       (prm[:, 0:8], sdtb_d[l, :].partition_broadcast(128)), (prm[:, 8:16], sal_d[l, :].partition_broadcast(128)),
                       (prm[:, 16:24], sD_d[l, :].partition_broadcast(128)), (prm[:, 24:28], sinks_d[l, :].partition_broadcast(128)),
                       (prm[:, 28:32], gdtb_d[l, :].partition_broadcast(128)), (prm[:, 32:36], gal_d[l, :].partition_broadcast(128)),
                       (nwS[:], snw_d[l, :].partition_broadcast(128)), (nwG[:], gnw_d[l, :].partition_broadcast(128)),
                       (dmat[:], c_dm)],
               writes=["g_pm", "g_po", "cw", "prm", "nwS", "nwG", "dmat"])
        em.dma("prmb", [(cbrow[:], scb_d[l:l + 1, :])], writes=["cbrow"], eng="pool")
        act(drv[:, 0:8], prm[:, 8:16], AF.Exp, ["prm"], ["drv"])
        act(drv[:, 8:12], prm[:, 32:36], AF.Exp, ["prm"], ["drv"])
        act(drv[:, 12:16], prm[:, 24:28], AF.Exp, ["prm"], ["drv"])
        ts("pool", drv[:, 0:12], drv[:, 0:12], -1.0, None, ALU.mult, None, ["drv"], ["drv"])
        tt("pool", dW[:], bc(ident_f[:], [("b", 56), "k"]),
           bc(cw[:].rearrange("p a b -> p (a b)"), ["k", ("b", 128)]), ALU.mult, ["ident_f", "cw"], ["dW"])

    FA, FB, SA, SB, GA, GB, GC = BK[1], BK[2], BK[3], BK[4], BK[5], BK[6], BK[7]

    def silu_g(src, out, tmp, tmpk, r, w):
        act(tmp, src, AF.Exp, r, [tmpk], scale=-1.0)
        yield
        act(tmp, tmp, AF.Ln, [tmpk], [tmpk], bias=1.0)
        yield
        act(tmp, tmp, AF.Exp, [tmpk], [tmpk], scale=-1.0)
        yield
        tt("dve", out, src, tmp, ALU.mult, r + [tmpk], w)
        yield

    def front(l, t, xsrc, xsrc_key):
        sl = t % 2
        s3 = t % 3
        xtile, xkey = xt[s3], f"xt{s3}"
        em.dma(f"xl{s3}", [(xtile[:], xsrc[t * 128:(t + 1) * 128, :])], reads=[f"{xsrc_key}{t}"], writes=[xkey])
        yield
        act(sqj[:], xtile[:], AF.Square, [xkey], ["sqj", "stat0"], accum=stat[:, 0:1])
        yield
        rsqrt_cols(stat[:, 0:1], 1.0 / D, ["stat0"], ["stat0"])
        yield
        stt("dve", hb[:], xtile[:], stat[:, 0:1], g_pm[:], ALU.mult, ALU.mult, [xkey, "stat0", "g_pm"], ["hb"])
        yield
        em.ops("pe", [tr(B0[:, k * 128:(k + 1) * 128], hb[:, k * 128:(k + 1) * 128]) for k in range(8)],
               ["hb", "ident_b"], ["B0"])
        cp("act", hT[:], B0[:].rearrange("p (k c) -> p k c", k=8), ["B0"], ["hT0"])
        yield
        wk = [f"win{k}" for k in range(8)]
        groups = [(0, 3), (3, 7), (7, 11), (11, 15), (15, 17)]
        if t > 0:
            cp("pool", xfm[:, :, 0:3], xfm[:, :, 128:131], ["xfm"], ["xfm"])
        for gi, (b0, b1) in enumerate(groups):
            bank, bkey = (FA, "B1") if gi % 2 == 0 else (FB, "B2")
            fns = []
            for j, b in enumerate(range(b0, b1)):
                for k in range(8):
                    fns.append(mm(bank[:, j * 128:(j + 1) * 128], w_in_sb[:, k, b * 128:(b + 1) * 128], hT[:, k, :],
                                  start=(k == 0), stop=(k == 7)))
            em.ops("pe", fns, ["hT0"] + wk, [bkey])
            yield
            if gi == 0:
                cp("act", qT[sl][:], bank[:, 0:256].rearrange("p (a c) -> p a c", a=2), [bkey], [f"qT{sl}"])
                cp("dve", kT[s3][:], bank[:, 256:384], [bkey], [f"kT{s3}"])
            else:
                nb = b1 - b0
                cp("act" if gi % 2 else "dve", xfm[:, b0 - 3:b1 - 3, 3:131],
                   bank[:, 0:nb * 128].rearrange("p (a c) -> p a c", a=nb), [bkey], ["xfm"])
            yield
        em.ops("pe", [mm(FB[:, 0:512], hT[:, k, :], w_in_sb[:, k, C_Z:C_Z + 512], start=(k == 0), stop=(k == 7))
                      for k in range(8)], ["hT0"] + wk, ["B2"])
        yield
        yield from silu_g(FB[:, 0:512], zs[sl][:], etmp[0][:], "etmp0", ["B2"], [f"zs{sl}"])
        em.ops("pe", [mm(FA[:, 0:400], hT[:, k, :], w_in_sb[:, k, C_AV:C_AV + 400], start=(k == 0), stop=(k == 7))
                      for k in range(8)], ["hT0"] + wk, ["B1"])
        yield
        cp("act", vaug[s3][:, :, 0:64], FA[:, 0:128].rearrange("p (a c) -> p a c", a=2), ["B1"], [f"vaug{s3}"])
        cp("dve", sm16[sl][:], FA[:, 384:400], ["B1"], [f"sm16_{sl}"])
        yield
        yield from silu_g(FA[:, 128:384], zsg[sl][:], etmp[1][:, 0:256], "etmp1", ["B1"], [f"zsg{sl}"])
        cgroups = [(0, 4), (4, 8), (8, 12), (12, 14)]
        for gi, (b0, b1) in enumerate(cgroups):
            bank, bkey = (FB, "B2") if gi % 2 == 0 else (FA, "B1")
            fns = []
            for j, b in enumerate(range(b0, b1)):
                o = bank[:, j * 128:(j + 1) * 128]
                for tap in range(4):
                    fns.append(mm(o, dW[:, b * 4 + tap, :], xfm[:, b, tap:tap + 128], start=(tap == 0),
                                  stop=(tap == 3 and b >= 8)))
                if b < 8:
                    fns.append(mm(o, cbrow[0:1, b * 128:(b + 1) * 128], ones_b[0:1, :], start=False, stop=True))
            em.ops("pe", fns, ["xfm", "dW", "cbrow", "ones_b"], [bkey])
            yield
            nb = b1 - b0
            yield from silu_g(bank[:, 0:nb * 128], cvs[sl][:, b0:b1, :].rearrange("p a c -> p (a c)"),
                              etmp[gi % 2][:, 0:nb * 128], f"etmp{gi % 2}", [bkey], [f"cvs{sl}"])
        em.ops("pe", [tr(B0[:, j * 128:(j + 1) * 128], cvs[sl][:, j, :]) for j in range(6)], [f"cvs{sl}", "ident_b"], ["B0"])
        cp("act", tm1[sl][:], B0[:, 0:768], ["B0"], [f"tm1_{sl}"])
        yield
        em.ops("pe", [tr(B0[:, j * 128:(j + 1) * 128], cvs[sl][:, 8 + j, :]) for j in range(6)], [f"cvs{sl}", "ident_b"], ["B0"])
        cp("act", tm2[sl][:], B0[:, 0:768], ["B0"], [f"tm2_{sl}"])
        yield

    def attn_ssd(l, t):
        sl = t % 2
        s3, p3 = t % 3, (t - 1) % 3
        kcur, kprev, vcur, vprev = kT[s3], kT[p3], vaug[s3], vaug[p3]
        kkeys = [f"kT{s3}", f"kT{p3}"]
        vkeys = [f"vaug{s3}", f"vaug{p3}"]
        heads = [(0, 0, 0), (1, 1, 0), (2, 0, 1), (3, 1, 1)]
        srcs = [1] if t == 0 else [0, 1]
        sbank = [SA, SB]
        for hf in range(2):
            fns = []
            for h, blk, half in heads:
                if half != hf:
                    continue
                for s_ in srcs:
                    kk = kprev if s_ == 0 else kcur
                    c0 = ((h % 2) * 2 + s_) * 128
                    fns.append(mm(sbank[h // 2][:, c0:c0 + 128], kk[half * 64:(half + 1) * 64, :],
                                  qT[sl][half * 64:(half + 1) * 64, blk, :]))
            em.ops("pe", fns, [f"qT{sl}"] + kkeys, ["B3", "B4"], after_self=True)
        yield
        for i in range(2):
            if t == 0:
                src_ap = sbank[i][:, 0:512].rearrange("p (a s c) -> p a s c", a=2, s=2)[:, :, 1, :]
                dst_ap = ex[:, i * 512:(i + 1) * 512].rearrange("p (a s c) -> p a s c", a=2, s=2)[:, :, 1, :]
            else:
                src_ap = sbank[i][:, 0:512]
                dst_ap = ex[:, i * 512:(i + 1) * 512]
            act(dst_ap, src_ap, AF.Exp, [f"B{3 + i}"], ["ex"], scale=0.125)
        yield
        if t == 0:
            v4 = lambda a: a.rearrange("p (a s c) -> p a s c", a=4, s=2)[:, :, 1, :]
            tt("pool", v4(pT[:]), v4(ex[:]), v4(dmat[:]), ALU.mult, ["ex", "dmat"], ["pT"])
        else:
            tt("pool", pT[:], ex[:], dmat[:], ALU.mult, ["ex", "dmat"], ["pT"])
        yield
        fns = []
        for h, blk, half in heads:
            for si, s_ in enumerate(srcs):
                vv = vprev if s_ == 0 else vcur
                c0 = (h * 2 + s_) * 128
                fns.append(mm(SA[:, h * 128:h * 128 + 65], pT[:, c0:c0 + 128], vv[:, half, 0:65],
                              start=(si == 0), stop=(si == len(srcs) - 1)))
        em.ops("pe", fns, ["pT"] + vkeys, ["B3"])
        yield
        o4 = SA[:, 0:512].rearrange("p (a c) -> p a c", a=4)
        tt("dve", att4[:, 0:4], o4[:, :, 64], drv[:, 12:16], ALU.add, ["B3", "drv"], ["att4"])
        yield
        em.op("dve", lambda: nc.vector.reciprocal(att4[:, 4:8], att4[:, 0:4]), ["att4"], ["att4"])
        yield
        tt("dve", cat[:, 0:256].rearrange("p (a c) -> p a c", a=4), o4[:, :, 0:64], bc(att4[:, 4:8], ["k", ("b", 64)]),
           ALU.mult, ["B3", "att4"], ["cat_a"])
        yield
        tm, tmk, cv, cvk, smk = tm1[sl], f"tm1_{sl}", cvs[sl], f"cvs{sl}", f"sm16_{sl}"
        xs3 = tm[:, 0:512].rearrange("p (h c) -> p h c", h=8)
        tt("dve", s8[:, 0:8], sm16[sl][:, 0:8], prm[:, 0:8], ALU.add, [smk, "prm"], ["s8a"])
        yield
        act(s8[:, 0:8], s8[:, 0:8], AF.Exp, ["s8a"], ["s8a"])
        yield
        act(s8[:, 0:8], s8[:, 0:8], AF.Ln, ["s8a"], ["s8a"], bias=1.0)
        yield
        tt("dve", s8[:, 8:16], s8[:, 0:8], drv[:, 0:8], ALU.mult, ["s8a", "drv"], ["s8b"])
        yield
        em.ops("pe", [mm(SB[:, 0:8], U_f[:], s8[:, 8:16]), mm(SB[:, 8:16], ones_f[:], s8[:, 8:16])],
               ["U_f", "ones_f", "s8b"], ["B4"])
        tt("pool", lhsTd[:], bc(SL_f[:], [("b", 8), "k"]), bc(s8[:, 8:16], ["k", ("b", 128)]), ALU.mult,
           ["SL_f", "s8b"], ["lhsTd"])
        yield
        cp("act", s8[:, 16:32], SB[:, 0:16], ["B4"], ["s8c"])
        yield
        em.ops("pe", [mm((SA if h < 4 else SB)[:, (h % 4) * 128:(h % 4 + 1) * 128], lhsTd[:, h, :], U_f[:]) for h in range(8)],
               ["lhsTd", "U_f"], ["B3", "B4"])
        act(s8[:, 32:40], s8[:, 16:24], AF.Exp, ["s8c"], ["s8d"])
        tt("dve", s8[:, 40:48], s8[:, 24:32], s8[:, 16:24], ALU.subtract, ["s8c"], ["s8e"])
        yield
        act(s8[:, 40:48], s8[:, 40:48], AF.Exp, ["s8e"], ["s8e"])
        act(s8[:, 48:56], s8[:, 24:32], AF.Exp, ["s8c"], ["s8f"])
        yield
        act(LT[:, 0:4, :].rearrange("p a c -> p (a c)"), SA[:, 0:512], AF.Exp, ["B3"], ["LT"])
        act(LT[:, 4:8, :].rearrange("p a c -> p (a c)"), SB[:, 0:512], AF.Exp, ["B4"], ["LT"])
        tt("dve", xc[:].rearrange("p (h c) -> p h c", h=8), xs3, bc(s8[:, 0:8], ["k", ("b", 64)]), ALU.mult,
           [tmk, "s8a"], ["xc"])
        tt("pool", y2[:].rearrange("p (h c) -> p h c", h=8), xs3, bc(prm[:, 16:24], ["k", ("b", 64)]), ALU.mult,
           [tmk, "prm"], ["y2"])
        yield
        em.ops("pe", [mm(SA[:, g * 128:(g + 1) * 128], cv[:, 4 + g, :], cv[:, 6 + g, :]) for g in range(2)],
               [cvk], ["B3"])
        yield
        tt("dve", Gm[:], SA[:, 0:256].rearrange("p (g c) -> p g c", g=2), bc(U_f[:], [("b", 2), "k"]), ALU.mult,
           ["B3", "U_f"], ["Gm"])
        yield
        tt("pool", scT[:].rearrange("p (g a) c -> p g a c", g=2), LT[:].rearrange("p (g a) c -> p g a c", g=2),
           bc(Gm[:], ["k", ("b", 4), "k"]), ALU.mult, ["LT", "Gm"], ["scT"])
        tt("pool", xdec[:].rearrange("p (h c) -> p h c", h=8), xc[:].rearrange("p (h c) -> p h c", h=8),
           bc(s8[:, 40:48], ["k", ("b", 64)]), ALU.mult, ["xc", "s8e"], ["xdec"])
        yield
        em.ops("pe", [mm(SB[:, h * 64:(h + 1) * 64], scT[:, h, :], xc[:, h * 64:(h + 1) * 64]) for h in range(8)],
               ["scT", "xc"], ["B4"])
        if t > 0:
            em.ops("pe", [mm(SA[:, g * 256:(g + 1) * 256], cv[:, 6 + g, :], Sbf[:, g * 256:(g + 1) * 256]) for g in range(2)],
                   [cvk, "Sbf"], ["B3"])
        yield
        if t > 0:
            tt("dve", y1[:].rearrange("p (h c) -> p h c", h=8), SA[:, 0:512].rearrange("p (h c) -> p h c", h=8),
               bc(s8[:, 32:40], ["k", ("b", 64)]), ALU.mult, ["B3", "s8d"], ["y1"])
            yield
            tt("dve", y1[:], y1[:], SB[:, 0:512], ALU.add, ["y1", "B4"], ["y1"])
            yield
            tt("pool", y1[:], y1[:], y2[:], ALU.add, ["y1", "y2"], ["y1"])
        else:
            tt("dve", y1[:], y2[:], SB[:, 0:512], ALU.add, ["y2", "B4"], ["y1"])
        yield
        em.ops("pe", [mm(SB[:, g * 256:(g + 1) * 256], tm[:, 512 + g * 128:512 + (g + 1) * 128], xdec[:, g * 256:(g + 1) * 256])
                      for g in range(2)], [tmk, "xdec"], ["B4"])
        tt("dve", y1[:], y1[:], zs[sl][:], ALU.mult, ["y1", f"zs{sl}"], ["y1"])
        yield
        for g in range(2):
            act(sqj[:, g * 256:(g + 1) * 256], y1[:, g * 256:(g + 1) * 256], AF.Square, ["y1"], ["sqj", "s8g"],
                accum=s8[:, 56 + g:57 + g])
        if t > 0:
            tt("pool", S[:].rearrange("p (h c) -> p h c", h=8), S[:].rearrange("p (h c) -> p h c", h=8),
               bc(s8[:, 48:56], ["k", ("b", 64)]), ALU.mult, ["S", "s8f"], ["S"])
        yield
        rsqrt_cols(s8[:, 56:58], 1.0 / 256, ["s8g"], ["s8g"])
        if t > 0:
            tt("dve", S[:], S[:], SB[:, 0:512], ALU.add, ["S", "B4"], ["S"])
        else:
            cp("dve", S[:], SB[:, 0:512], ["B4"], ["S"])
        yield
        cp("pool", Sbf[:], S[:], ["S"], ["Sbf"])
        for g in range(2):
            stt("dve", cat[:, 256 + g * 256:512 + g * 256], y1[:, g * 256:(g + 1) * 256], s8[:, 56 + g:57 + g],
                nwS[:, g * 256:(g + 1) * 256], ALU.mult, ALU.mult, ["y1", "s8g", "nwS"], ["cat_s"])
        yield

    def gdn(l, t):
        sl = t % 2
        tm, tmk, smk, zg, zgk = tm2[sl], f"tm2_{sl}", f"sm16_{sl}", zsg[sl], f"zsg{sl}"
        sm = sm16[sl]
        q3 = tm[:, 0:256].rearrange("p (h c) -> p h c", h=4)
        k3 = tm[:, 256:512].rearrange("p (h c) -> p h c", h=4)
        v3 = tm[:, 512:768].rearrange("p (h c) -> p h c", h=4)
        tt("dve", sqk[:], tm[:, 0:512], tm[:, 0:512], ALU.mult, [tmk], ["sqk"])
        act(g8[:, 8:12], sm[:, 8:12], AF.Exp, [smk], ["g8b"], scale=-1.0)
        tt("dve", g8[:, 12:16], sm[:, 12:16], prm[:, 28:32], ALU.add, [smk, "prm"], ["g8c"])
        yield
        em.op("dve", lambda: nc.vector.tensor_reduce(g8[:, 0:8], sqk[:].rearrange("p (h c) -> p h c", h=8), AX.X, ALU.add),
              ["sqk"], ["g8a"])
        act(g8[:, 12:16], g8[:, 12:16], AF.Exp, ["g8c"], ["g8c"])
        yield
        act(g8[:, 12:16], g8[:, 12:16], AF.Ln, ["g8c"], ["g8c"], bias=1.0)
        ts("dve", g8[:, 8:12], g8[:, 8:12], 1.0, None, ALU.add, None, ["g8b"], ["g8b"])
        yield
        rsqrt_cols(g8[:, 0:8], 1.0, ["g8a"], ["g8a"])
        em.op("dve", lambda: nc.vector.reciprocal(g8[:, 8:12], g8[:, 8:12]), ["g8b"], ["g8b"])
        yield
        tt("dve", g8[:, 12:16], g8[:, 12:16], drv[:, 8:12], ALU.mult, ["g8c", "drv"], ["g8c"])
        yield
        em.ops("pe", [mm(GA[:, 0:4], U_f[:], g8[:, 12:16]), mm(GA[:, 4:8], ones_f[:], g8[:, 12:16])],
               ["U_f", "ones_f", "g8c"], ["B5"])
        tt("pool", lD, bc(U_f[:], [("b", 4), "k"]), bc(g8[:, 12:16], ["k", ("b", 128)]), ALU.mult, ["U_f", "g8c"], ["lDD"])
        tt("pool", lDT, bc(SL_f[:], [("b", 4), "k"]), bc(g8[:, 12:16], ["k", ("b", 128)]), ALU.mult, ["SL_f", "g8c"], ["lDD"])
        yield
        cp("act", g8[:, 16:24], GA[:, 0:8], ["B5"], ["g8d"])
        em.ops("pe", [mm(GB[:, h * 128:(h + 1) * 128], lD[:, h, :], SL_f[:]) for h in range(4)]
               + [mm(GC[:, h * 128:(h + 1) * 128], lDT[:, h, :], U_f[:]) for h in range(4)],
               ["lDD", "SL_f", "U_f"], ["B6", "B7"])
        yield
        act(g8[:, 24:28], g8[:, 16:20], AF.Exp, ["g8d"], ["g8e"])
        tt("dve", g8[:, 28:32], g8[:, 20:24], g8[:, 16:20], ALU.subtract, ["g8d"], ["g8f"])
        yield
        act(g8[:, 28:32], g8[:, 28:32], AF.Exp, ["g8f"], ["g8f"])
        act(g8[:, 32:36], g8[:, 20:24], AF.Exp, ["g8d"], ["g8g"])
        yield
        act(eD[:], GB[:, 0:512], AF.Exp, ["B6"], ["eD"])
        act(eDT[:], GC[:, 0:512], AF.Exp, ["B7"], ["eDT"])
        cp("dve", sK[:, 0, :], g8[:, 4:8], ["g8a"], ["sK"])
        tt("dve", sK[:, 1, :], g8[:, 4:8], g8[:, 8:12], ALU.mult, ["g8a", "g8b"], ["sK"])
        yield
        tt("dve", sK[:, 2, :], sK[:, 1, :], g8[:, 24:28], ALU.mult, ["sK", "g8e"], ["sK"])
        tt("dve", sK[:, 3, :], g8[:, 4:8], g8[:, 28:32], ALU.mult, ["g8a", "g8f", "sK"], ["sK"])
        ts("dve", sQ[:, 0, :], g8[:, 0:4], 0.125, None, ALU.mult, None, ["g8a"], ["sQ"])
        yield
        tt("dve", sQ[:, 1, :], sQ[:, 0, :], g8[:, 24:28], ALU.mult, ["sQ", "g8e"], ["sQ"])
        tt("pool", Kv[:].rearrange("p v (h c) -> p v h c", h=4), bc(k3, [("b", 4), "k", "k"]), bc(sK[:], ["k", "k", ("b", 64)]),
           ALU.mult, [tmk, "sK"], ["Kv"])
        yield
        tt("dve", Qv[:].rearrange("p v (h c) -> p v h c", h=4), bc(q3, [("b", 2), "k", "k"]), bc(sQ[:], ["k", "k", ("b", 64)]),
           ALU.mult, [tmk, "sQ"], ["Qv"])
        tt("pool", vb[:].rearrange("p (h c) -> p h c", h=4), v3, bc(g8[:, 8:12], ["k", ("b", 64)]), ALU.mult,
           [tmk, "g8b"], ["vb"])
        m4 = lambda a: a.rearrange("p (h c) -> p h c", h=4)
        tt("pool", m4(eD[:]), m4(eD[:]), bc(SL_f[:], [("b", 4), "k"]), ALU.mult, ["eD", "SL_f"], ["eD"])
        yield
        tt("pool", m4(eDT2[:]), m4(eDT[:]), bc(U_f[:], [("b", 4), "k"]), ALU.mult, ["eDT", "U_f"], ["eDT2"])
        tt("pool", m4(eDT[:]), m4(eDT[:]), bc(US_f[:], [("b", 4), "k"]), ALU.mult, ["eDT", "US_f", "eDT2"], ["eDT"])
        yield
        srcT = [Kv[:, 0, 0:128], Kv[:, 0, 128:256], Kv[:, 1, 0:128], Kv[:, 1, 128:256],
                Qv[:, 0, 0:128], Qv[:, 0, 128:256], Qv[:, 1, 0:128], Qv[:, 1, 128:256]]
        em.ops("pe", [tr(B0[:, j * 128:(j + 1) * 128], srcT[j]) for j in range(8)], ["Kv", "Qv", "ident_b"], ["B0"])
        cp("act", FMg[:], B0[:].rearrange("p (k c) -> p k c", k=8), ["B0"], ["FMg"])
        yield

        def fm(var, h):
            pair, half = h // 2, h % 2
            return FMg[half * 64:(half + 1) * 64, var * 2 + pair, :]

        for half in range(2):
            fns = []
            for bank, (va, vb_) in ((GA, (1, 0)), (GB, (0, 1)), (GC, (0, 2))):
                for pair in range(2):
                    h = pair * 2 + half
                    fns.append(mm(bank[:, h * 128:(h + 1) * 128], fm(va, h), fm(vb_, h)))
            em.ops("pe", fns, ["FMg"], ["B5", "B6", "B7"], after_self=True)
        yield
        f4 = lambda a: a.rearrange("p h c -> p (h c)")
        tt("dve", f4(XmF[:]), GA[:, 0:512], eD[:], ALU.mult, ["B5", "eD"], ["XmF"])
        yield
        tt("dve", f4(YmF[:]), GB[:, 0:512], eDT[:], ALU.mult, ["B6", "eDT"], ["YmF"])
        yield
        tt("dve", f4(QKm[:]), GC[:, 0:512], eDT2[:], ALU.mult, ["B7", "eDT2"], ["QKm"])
        tt("pool", TTF[:], bc(ident_f[:], [("b", 4), "k"]), YmF[:], ALU.subtract, ["ident_f", "YmF"], ["TTF"])
        yield
        for lev in range(6):
            last = lev == 5
            em.ops("pe", [mm(GA[:, h * 128:(h + 1) * 128], YmF[:, h, :], XmF[:, h, :]) for h in range(4)],
                   ["XmF", "YmF"], ["B5"])
            if not last:
                em.ops("pe", [mm(GB[:, h * 128:(h + 1) * 128], XmF[:, h, :], YmF[:, h, :]) for h in range(4)],
                       ["XmF", "YmF"], ["B6"])
            yield
            cp("act", f4(XmF[:]), GA[:, 0:512], ["B5"], ["XmF"])
            if not last:
                cp("dve", f4(YmF[:]), GB[:, 0:512], ["B6"], ["YmF"])
            yield
            em.ops("pe", [mm(GC[:, h * 128:(h + 1) * 128], XmF[:, h, :], TTF[:, h, :]) for h in range(4)],
                   ["XmF", "TTF"], ["B7"])
            yield
            tt("dve", f4(TTF[:]), f4(TTF[:]), GC[:, 0:512], ALU.add, ["TTF", "B7"], ["TTF"])
            yield
        cp("pool", TT, TTF[:], ["TTF", "XmF"], ["TT", "XmF"])
        yield
        em.ops("pe", [mm(GA[:, h * 128:(h + 1) * 128], Kv[:, 2, (h // 2) * 128:(h // 2 + 1) * 128], TT[:, h, :]) for h in range(4)],
               ["Kv", "TT"], ["B5"])
        yield
        for half in range(2):
            src_ap = GA[half * 64:(half + 1) * 64, 0:512].rearrange("p (pr hh c) -> p pr hh c", pr=2, hh=2)[:, :, half, :]
            em.op("act", lambda src_ap=src_ap, half=half: nc.scalar.mul(negwT[half * 64:(half + 1) * 64, :, :], src_ap, -1.0),
                  ["B5"], ["negwT"])
        yield
        for h in range(4):
            pair, half = h // 2, h % 2
            o = GB[:, h * 64:(h + 1) * 64]
            fns = [mm(o, TT[:, h, :], vb[:, h * 64:(h + 1) * 64], start=True, stop=(t == 0))]
            if t > 0:
                fns.append(mm(o, negwT[half * 64:(half + 1) * 64, pair, :], Sgb[half * 64:(half + 1) * 64, pair, :],
                              start=False, stop=True))
            em.ops("pe", fns, ["TT", "vb", "negwT", "Sgb"], ["B6"], after_self=True)
        yield
        cp("act", vnew[:], GB[:, 0:256], ["B6"], ["vnew"])
        yield
        for h in range(4):
            pair, half = h // 2, h % 2
            o = GC[:, h * 64:(h + 1) * 64]
            fns = [mm(o, QKm[:, h, :], vnew[:, h * 64:(h + 1) * 64], start=True, stop=(t == 0))]
            if t > 0:
                fns.append(mm(o, fm(3, h), Sgb[half * 64:(half + 1) * 64, pair, :], start=False, stop=True))
            em.ops("pe", fns, ["QKm", "vnew", "FMg", "Sgb"], ["B7"], after_self=True)
        em.ops("pe", [mm(GA[:, pr * 128:(pr + 1) * 128], Kv[:, 3, pr * 128:(pr + 1) * 128], vnew[:, pr * 128:(pr + 1) * 128])
                      for pr in range(2)], ["Kv", "vnew", "negwT"], ["B5"], after_self=True)
        yield
        cp("act", osb[:], GC[:, 0:256], ["B7"], ["osb"])
        for h in range(4):
            pair, half = h // 2, h % 2
            rows = slice(half * 64, (half + 1) * 64)
            blk = GA[rows, pair * 128 + half * 64:pair * 128 + (half + 1) * 64]
            if t > 0:
                stt("dve", Sg[rows, pair, :], Sg[rows, pair, :], g8[rows, 32 + h:33 + h], blk, ALU.mult, ALU.add,
                    ["Sg", "g8g", "B5", "Sgb"], ["Sg"])
            else:
                cp("dve", Sg[rows, pair, :], blk, ["B5"], ["Sg"])
        yield
        cp("pool", Sgb[:], Sg[:], ["Sg"], ["Sgb"])
        tt("pool", sqo[:], osb[:], osb[:], ALU.mult, ["osb"], ["sqo"])
        yield
        em.op("dve", lambda: nc.vector.tensor_reduce(g8[:, 36:40], sqo[:].rearrange("p (h c) -> p h c", h=4), AX.X, ALU.add),
              ["sqo"], ["g8h"])
        tt("pool", zg[:].rearrange("p (h c) -> p h c", h=4), zg[:].rearrange("p (h c) -> p h c", h=4),
           bc(nwG[:], [("b", 4), "k"]), ALU.mult, [zgk, "nwG"], [zgk])
        yield
        rsqrt_cols(g8[:, 36:40], 1.0 / 64, ["g8h"], ["g8h"])
        yield
        for h in range(4):
            stt("dve", cat[:, 768 + h * 64:768 + (h + 1) * 64], osb[:, h * 64:(h + 1) * 64], g8[:, 36 + h:37 + h],
                zg[:, h * 64:(h + 1) * 64], ALU.mult, ALU.mult, ["osb", "g8h", zgk], ["cat_g"])
        yield

    def back(l, t, xdst, xdst_key):
        s3 = t % 3
        if dbg_d is not None and l == 0 and t == 1:
            cp("dve", ex[:], cat[:], ["cat_a", "cat_s", "cat_g"], ["ex"])
            em.dma("dbg", [(dbg_d, ex[:])], reads=["ex"], writes=["dbgout"])
        em.ops("pe", [tr(B0[:, k * 128:(k + 1) * 128], cat[:, k * 128:(k + 1) * 128]) for k in range(8)],
               ["cat_a", "cat_s", "cat_g", "ident_b"], ["B0"])
        cp("act", catT[:], B0[:].rearrange("p (k c) -> p k c", k=8), ["B0"], ["catT"])
        yield
        for nb in range(2):
            em.ops("pe", [mm(BK[3 + nb][:, 0:512], catT[:, k, :], w_out_sb[:, k, nb * 512:(nb + 1) * 512], start=(k == 0), stop=(k == 7))
                          for k in range(8)], ["catT"] + [f"wout{k}" for k in range(8)], [f"B{3 + nb}"])
        yield
        for i, (b, bk) in enumerate(((SA, "B3"), (SB, "B4"))):
            act(sqj[:, 0:512], b[:, 0:512], AF.Square, [bk], ["sqj", f"stat{1 + i}"], accum=stat[:, 1 + i:2 + i])
        yield
        tt("dve", stat[:, 3:4], stat[:, 1:2], stat[:, 2:3], ALU.add, ["stat1", "stat2"], ["stat3"])
        yield
        rsqrt_cols(stat[:, 3:4], 1.0 / D, ["stat3"], ["stat3"])
        yield
        for i, (b, bk) in enumerate(((SA, "B3"), (SB, "B4"))):
            stt("dve", xo[:, i * 512:(i + 1) * 512], b[:, 0:512], stat[:, 3:4], g_po[:, i * 512:(i + 1) * 512],
                ALU.mult, ALU.mult, [bk, "stat3", "g_po"], [f"xo{i}"])
            yield
            tt("pool", xt[s3][:, i * 512:(i + 1) * 512], xo[:, i * 512:(i + 1) * 512], xt[s3][:, i * 512:(i + 1) * 512],
               ALU.add, [f"xo{i}", f"xt{s3}"], [f"xt{s3}"])
            yield
        em.dma(f"xs{s3}", [(xdst[t * 128:(t + 1) * 128, :], xt[s3][:])], reads=[f"xt{s3}"], writes=[f"{xdst_key}{t}"])
        yield

    def thread2(l, t, xdst, xdst_key):
        if t > 0:
            yield from back(l, t - 1, xdst, xdst_key)
        yield from attn_ssd(l, t)

    def run_threads(gens):
        gens = list(gens)
        while gens:
            for g_ in list(gens):
                try:
                    next(g_)
                except StopIteration:
                    gens.remove(g_)

    def ffn_front(t, xsrc, xsrc_key):
        sl, s3 = t % 2, t % 3
        em.dma(f"xl{s3}", [(xt[s3][:], xsrc[t * 128:(t + 1) * 128, :])], reads=[f"{xsrc_key}{t}"], writes=[f"xt{s3}"])
        rmsnorm_to_hT(xt[s3], f"xt{s3}", g_pf, "g_pf", 0, hT2[sl], f"hT{sl}")

    def ffn_tile(l, t, xsrc, xsrc_key, xdst, xdst_key):
        sl = t % 2
        hTt, hkey = hT2[sl], f"hT{sl}"
        nun = (NFB + 3) // 4
        for u in range(nun):
            f0, f1 = u * 4, min(NFB, u * 4 + 4)
            nb = f1 - f0
            w = nb * 128
            gbank, ubank = (BK[1], BK[2]) if u % 2 == 0 else (BK[3], BK[4])
            gk, uk = ("B1", "B2") if u % 2 == 0 else ("B3", "B4")
            fns = []
            for j, fb in enumerate(range(f0, f1)):
                for k in range(8):
                    fns.append(mm(gbank[:, j * 128:(j + 1) * 128], wg_sb[:, k, fb * 128:(fb + 1) * 128], hTt[:, k, :],
                                  start=(k == 0), stop=(k == 7)))
                for k in range(8):
                    fns.append(mm(ubank[:, j * 128:(j + 1) * 128], wu_sb[:, k, fb * 128:(fb + 1) * 128], hTt[:, k, :],
                                  start=(k == 0), stop=(k == 7)))
            em.ops("pe", fns, [hkey] + [f"wg{k}" for k in range(8)] + [f"wu{k}" for k in range(8)], [gk, uk])
            e, ek = etmp[u % 2][:, 0:w], f"etmp{u % 2}"
            sg, sk = sgt2[u % 2][:, 0:w], f"sgt{u % 2}"
            act(e, gbank[:, 0:w], AF.Exp, [gk], [ek], scale=-1.0)
            act(e, e, AF.Ln, [ek], [ek], bias=1.0)
            act(e, e, AF.Exp, [ek], [ek], scale=-1.0)
            tt("dve", sg, gbank[:, 0:w], e, ALU.mult, [gk, ek], [sk])
            tt("dve", hidT[:, f0:f1, :].rearrange("p a c -> p (a c)"), sg, ubank[:, 0:w], ALU.mult, [sk, uk], ["hidT"])
            if u == 1 and t + 1 < NT:
                ffn_front(t + 1, xsrc, xsrc_key)
        for nb_ in range(2):
            em.ops("pe", [mm(BK[6 + nb_][:, 0:512], hidT[:, fc, :], wd_sb[:, fc, nb_ * 512:(nb_ + 1) * 512],
                             start=(fc == 0), stop=(fc == NFB - 1)) for fc in range(NFB)],
                   ["hidT"] + [f"wd{k}" for k in range(NFB)], [f"B{6 + nb_}"])
        s3 = t % 3
        post_norm_residual([BK[6], BK[7]], ["B6", "B7"], xt[s3], f"xt{s3}", g_pof, "g_pof")
        em.dma(f"xs{s3}", [(xdst[t * 128:(t + 1) * 128, :], xt[s3][:])], reads=[f"xt{s3}"], writes=[f"{xdst_key}{t}"])

    try:
      for l in range(L):
        if l > 0:
            em.barrier()
        load_weights_M(l)
        load_params(l)
        for s_ in range(3):
            em.op("pool", lambda s_=s_: nc.gpsimd.memset(vaug[s_][:, :, 64:65], 1.0), [], [f"vaug{s_}"])
        em.op("pool", lambda: nc.gpsimd.memset(xfm[:, :, 0:3], 0.0), [], ["xfm"])
        xsrc, xkey = (x_d, "xin") if l == 0 else (xb_d, f"xb{l - 1}_")
        run_threads([front(l, 0, xsrc, xkey)])
        for t in range(NT):
            th = [gdn(l, t), thread2(l, t, xa_d, f"xa{l}_")]
            if t + 1 < NT:
                th.append(front(l, t + 1, xsrc, xkey))
            run_threads(th)
            chk("mix")
        run_threads([back(l, NT - 1, xa_d, f"xa{l}_")])
        em.barrier()
        load_weights_F(l)
        last = l == L - 1
        ffn_front(0, xa_d, f"xa{l}_")
        for t in range(NT):
            ffn_tile(l, t, xa_d, f"xa{l}_", out_d if last else xb_d, "out" if last else f"xb{l}_")
            chk("ffn")
    except _Stop:
        em.barrier()
    em.finish([f"out{t}" for t in range(NT)])
    em.emit()
    return nc, st, em


def _consts():
    k = np.arange(128)
    U = (k[:, None] <= k[None, :]).astype(np.float32)
    SLm = (k[:, None] > k[None, :]).astype(np.float32)
    US = (k[:, None] < k[None, :]).astype(np.float32)
    ident = np.eye(128, dtype=np.float32)
    slopes = 2.0 ** (-8.0 * np.arange(1, 5, dtype=np.float64) / 4)
    dm = np.zeros((128, 4, 2, 128), np.float64)
    for s_ in range(2):
        tk = k[:, None] + (s_ - 1) * 128
        tq = k[None, :]
        qc = tq // 64
        kc_ = np.floor_divide(tk, 64)
        valid = (kc_ >= qc - 2) & (kc_ <= qc)
        for h in range(4):
            dm[:, h, s_, :] = np.where(valid, np.exp(-slopes[h] * np.abs(tq - tk)), 0.0)
    return ident, U, SLm, US, dm.reshape(128, 1024).astype(np.float32)


def make_in_maps(inputs, NT, n_cores):
    L = inputs["w_in"].shape[0]
    perm = _win_perm()
    ident, U, SLm, US, dm = _consts()
    f = lambda a: np.ascontiguousarray(np.asarray(a, dtype=np.float32))
    shared = {
        "pre_mix_norm": f(inputs["pre_mix_norm"]), "post_mix_norm": f(inputs["post_mix_norm"]),
        "pre_ffn_norm": f(inputs["pre_ffn_norm"]), "post_ffn_norm": f(inputs["post_ffn_norm"]),
        "w_in": f(np.asarray(inputs["w_in"])[:, :, perm]), "w_out": f(inputs["w_out"]),
        "attn_sinks": f(inputs["attn_sinks"]),
        "ssd_conv_w": f(np.asarray(inputs["ssd_conv_w"]).reshape(L, 4, 8, 128).transpose(0, 3, 2, 1)),
        "ssd_conv_b": f(inputs["ssd_conv_b"]), "ssd_dt_bias": f(inputs["ssd_dt_bias"]),
        "ssd_A_log": f(inputs["ssd_A_log"]), "ssd_D": f(inputs["ssd_D"]), "ssd_norm_w": f(inputs["ssd_norm_w"]),
        "gdn_conv_w": f(np.asarray(inputs["gdn_conv_w"]).reshape(L, 4, 6, 128).transpose(0, 3, 2, 1)),
        "gdn_dt_bias": f(inputs["gdn_dt_bias"]), "gdn_A_log": f(inputs["gdn_A_log"]), "gdn_norm_w": f(inputs["gdn_norm_w"]),
        "ffn_w_gate": f(inputs["ffn_w_gate"]), "ffn_w_up": f(inputs["ffn_w_up"]), "ffn_w_down": f(inputs["ffn_w_down"]),
        "c_ident": ident, "c_U": U, "c_SL": SLm, "c_US": US, "c_dmat": dm,
    }
    x = np.asarray(inputs["x"], dtype=np.float32)
    nb = x.shape[0]
    maps = []
    for c in range(n_cores):
        m = dict(shared)
        m["x"] = np.ascontiguousarray(x[c % nb, :NT * 128])
        maps.append(m)
    return maps


def kernel(**inputs):
    x = np.asarray(inputs["x"])
    Bn, T, _ = x.shape
    NT = T // 128
    L = np.asarray(inputs["w_in"]).shape[0]
    nc, st, em = build_program(NT, L)
    maps = make_in_maps(inputs, NT, 8)
    res = run_bass_kernel_spmd(nc, maps, core_ids=list(range(8)))
    out = np.stack([np.asarray(res.results[b]["out"], dtype=np.float32) for b in range(Bn)], axis=0)
    return out
```

```python
import numpy as np
from contextlib import ExitStack
import concourse.bass as bass
import concourse.mybir as mybir
from concourse.ap import AP
from concourse.bass_utils import run_bass_kernel_spmd

F32 = mybir.dt.float32
BF16 = mybir.dt.bfloat16
AF = mybir.ActivationFunctionType
ALU = mybir.AluOpType
AX = mybir.AxisListType

EPOCH = 2048
D = 1024
FF = 2816
NFB = FF // 128
EPS = 1e-6


class Emit:
    ENG = ("pe", "act", "dve", "pool", "sp")

    def __init__(self, nc, stack):
        self.nc = nc
        self.stack = stack
        self.lists = {e: [] for e in self.ENG}
        self.cnt = {e: 0 for e in self.ENG}
        self.esems = {e: [] for e in self.ENG}
        self.dsems = {}
        self.waited = {e: {} for e in self.ENG}
        self.lastw = {}
        self.reads = {}
        self.ninst = 0

    def _newsem(self, name):
        return self.stack.enter_context(self.nc.semaphore(name))

    def _esem(self, eng, idx):
        lst = self.esems[eng]
        while len(lst) <= idx:
            lst.append(self._newsem(f"e_{eng}_{len(lst)}"))
        return lst[idx]

    def _engobj(self, eng):
        nc = self.nc
        return {"pe": nc.tensor, "act": nc.scalar, "dve": nc.vector, "pool": nc.gpsimd, "sp": nc.sync}[eng]

    def _deps(self, eng, reads, writes):
        deps = {}

        def add(ev, same_ok):
            semname, sem, val, weng = ev
            if weng == eng and same_ok:
                return
            if deps.get(semname, (None, 0))[1] < val:
                deps[semname] = (sem, val)

        for k in reads:
            ev = self.lastw.get(k)
            if ev is not None:
                add(ev, same_ok=(eng == "pe"))
        for k in writes:
            ev = self.lastw.get(k)
            if ev is not None:
                add(ev, same_ok=(eng == "pe"))
            for ev in self.reads.get(k, {}).values():
                add(ev, same_ok=(eng == "pe"))
        out = []
        for semname, (sem, val) in deps.items():
            if self.waited[eng].get(semname, 0) < val:
                self.waited[eng][semname] = val
                out.append((sem, val))
        return out

    def _record(self, ev, reads, writes):
        for k in writes:
            self.lastw[k] = ev
            self.reads[k] = {}
        for k in reads:
            self.reads.setdefault(k, {})[(ev[0], ev[3])] = ev

    def ops(self, eng, fns, reads=(), writes=(), clobbers=(), after_self=False):
        reads = list(reads)
        writes = list(writes)
        excl = [k for k in reads if k[0] == "B" and k[1:].isdigit()]
        waits = self._deps(eng, reads, writes + list(clobbers) + excl)
        if after_self and self.cnt[eng] > 0:
            pidx = (self.cnt[eng] - 1) // EPOCH
            pname = f"e_{eng}_{pidx}"
            pval = self.cnt[eng] - pidx * EPOCH
            if self.waited[eng].get(pname, 0) < pval:
                self.waited[eng][pname] = pval
                waits.append((self.esems[eng][pidx], pval))
        idx = self.cnt[eng] // EPOCH
        sem = self._esem(eng, idx)
        self.cnt[eng] += 1
        val = self.cnt[eng] - idx * EPOCH
        semname = f"e_{eng}_{idx}"
        e = self._engobj(eng)

        def run(waits=waits, fns=fns, sem=sem, e=e):
            for s, v in waits:
                e.wait_ge(s, v)
            for f in fns[:-1]:
                f()
            fns[-1]().then_inc(sem, 1)

        self.lists[eng].append(run)
        self.ninst += len(fns)
        self._record((semname, sem, val, eng), reads, writes)

    def op(self, eng, fn, reads=(), writes=(), clobbers=()):
        self.ops(eng, [fn], reads, writes, clobbers)

    def dma(self, semname, pairs, reads=(), writes=(), eng="sp", clobbers=()):
        reads = list(reads)
        writes = list(writes)
        waits = self._deps(eng, reads, writes + list(clobbers))
        if semname not in self.dsems:
            self.dsems[semname] = [self._newsem("d_" + semname), 0]
        ent = self.dsems[semname]
        sem = ent[0]
        ent[1] += 16 * len(pairs)
        val = ent[1]
        e = self._engobj(eng)

        def run(waits=waits, pairs=pairs, sem=sem, e=e):
            for s, v in waits:
                e.wait_ge(s, v)
            for o, i in pairs:
                e.dma_start(out=o, in_=i).then_inc(sem, 16)

        self.lists[eng].append(run)
        self.ninst += len(pairs)
        self._record(("d_" + semname, sem, val, "dma"), reads, writes)

    def barrier(self):
        targets = []
        for e in self.ENG:
            if self.cnt[e] > 0:
                idx = (self.cnt[e] - 1) // EPOCH
                targets.append((f"e_{e}_{idx}", self.esems[e][idx], self.cnt[e] - idx * EPOCH, e))
        for name, (sem, val) in self.dsems.items():
            if val > 0:
                targets.append(("d_" + name, sem, val, "dma"))
        for eng in self.ENG:
            waits = []
            for semname, sem, val, weng in targets:
                if weng == eng:
                    continue
                if self.waited[eng].get(semname, 0) < val:
                    self.waited[eng][semname] = val
                    waits.append((sem, val))
            e = self._engobj(eng)

            def run(waits=waits, e=e):
                for s_, v in waits:
                    e.wait_ge(s_, v)

            self.lists[eng].append(run)

    def finish(self, final_keys):
        waits = self._deps("sp", [], list(final_keys))
        e = self._engobj("sp")

        def run():
            for s, v in waits:
                e.wait_ge(s, v)

        self.lists["sp"].append(run)

    def emit(self):
        nc = self.nc
        with nc.Block() as block:
            @block.sync
            def _(eng):
                for f in self.lists["sp"]:
                    f()

            @block.tensor
            def _(eng):
                for f in self.lists["pe"]:
                    f()

            @block.scalar
            def _(eng):
                for f in self.lists["act"]:
                    f()

            @block.vector
            def _(eng):
                for f in self.lists["dve"]:
                    f()

            @block.gpsimd
            def _(eng):
                for f in self.lists["pool"]:
                    f()


def bc(ap, dims):
    src = list(ap.ap)
    out = [list(src[0])]
    it = iter(src[1:])
    for d in dims:
        if d == "k":
            s = next(it)
            out.append([s[0], s[1]])
        else:
            out.append([0, d[1]])
    return AP(ap.tensor, ap.offset, out)


def _win_perm():
    segs = [(0, 64), (128, 64), (64, 64), (192, 64), (256, 128), (1024, 1024), (2056, 768),
            (512, 512), (384, 128), (2824, 256), (2048, 8), (3080, 8)]
    return np.concatenate([np.arange(s, s + n) for s, n in segs])


C_TM = 17 * 128
C_Z, C_AV, C_GZ, C_SM = C_TM, C_TM + 512, C_TM + 640, C_TM + 896


class _Stop(Exception):
    pass


def build_program(NT, L, dbg_stop=None):
    T = NT * 128
    FT = 1
    NFT = NT // FT
    nc = bass.Bass("TRN2", target_bir_lowering=False)
    st = ExitStack()
    em = Emit(nc, st)

    def din(name, shape):
        return nc.dram_tensor(name, list(shape), F32, kind="ExternalInput").ap()

    x_d = din("x", [T, D])
    n_pm, n_po, n_pf, n_pof = (din(n, [L, D]) for n in ("pre_mix_norm", "post_mix_norm", "pre_ffn_norm", "post_ffn_norm"))
    w_in_d = din("w_in", [L, D, 3088])
    w_out_d = din("w_out", [L, D, D])
    sinks_d = din("attn_sinks", [L, 4])
    scw_d = din("ssd_conv_w", [L, 128, 8, 4])
    scb_d = din("ssd_conv_b", [L, 1024])
    sdtb_d = din("ssd_dt_bias", [L, 8])
    sal_d = din("ssd_A_log", [L, 8])
    sD_d = din("ssd_D", [L, 8])
    snw_d = din("ssd_norm_w", [L, 512])
    gcw_d = din("gdn_conv_w", [L, 128, 6, 4])
    gdtb_d = din("gdn_dt_bias", [L, 4])
    gal_d = din("gdn_A_log", [L, 4])
    gnw_d = din("gdn_norm_w", [L, 64])
    wg_d = din("ffn_w_gate", [L, D, FF])
    wu_d = din("ffn_w_up", [L, D, FF])
    wd_d = din("ffn_w_down", [L, FF, D])
    c_id = din("c_ident", [128, 128])
    c_U = din("c_U", [128, 128])
    c_SL = din("c_SL", [128, 128])
    c_US = din("c_US", [128, 128])
    c_dm = din("c_dmat", [128, 1024])
    out_d = nc.dram_tensor("out", [T, D], F32, kind="ExternalOutput").ap()
    dbg_d = nc.dram_tensor("dbg", [128, D], F32, kind="ExternalOutput").ap() if dbg_stop == "dbgcat" else None
    xa_d = nc.dram_tensor("xa_scr", [T, D], F32).ap()
    xb_d = nc.dram_tensor("xb_scr", [T, D], F32).ap()

    def sb(name, shape, dt=F32):
        return st.enter_context(nc.sbuf_tensor(name, list(shape), dt))[:]

    B0 = st.enter_context(nc.psum_tensor("B0", [128, 1024], BF16))[:]
    BK = [None] + [st.enter_context(nc.psum_tensor(f"B{i}", [128, 512], F32))[:] for i in range(1, 8)]

    ACOLS = 3 * 8 * FF
    arena = sb("arena", [128, ACOLS], BF16)
    w_in_sb = arena[:, 0:8 * 3088].rearrange("p (k c) -> p k c", k=8)
    w_out_sb = arena[:, 8 * 3088:8 * 3088 + 8 * 1024].rearrange("p (k c) -> p k c", k=8)
    wg_sb = arena[:, 0:8 * FF].rearrange("p (k c) -> p k c", k=8)
    wu_sb = arena[:, 8 * FF:16 * FF].rearrange("p (k c) -> p k c", k=8)
    wd_sb = arena[:, 16 * FF:16 * FF + NFB * 1024].rearrange("p (k c) -> p k c", k=NFB)
    WM_KEYS = [f"win{k}" for k in range(8)] + [f"wout{k}" for k in range(8)]
    WF_KEYS = [f"wg{k}" for k in range(8)] + [f"wu{k}" for k in range(8)] + [f"wd{k}" for k in range(NFB)]
    tail = [8 * 3088 + 8 * 1024]
    AUXC = NFB * 128 + 2 * 1024 + 1024 + 2 * 2048
    aux = sb("aux", [128, AUXC], BF16)
    auxm = [0]
    auxf = [0]

    def carve(region, cur, limit, shape, dt):
        n = int(np.prod(shape[1:]))
        cols = n * (2 if dt == F32 else 1)
        cols += cols % 2
        if cur[0] + cols > limit:
            return None
        v = region[:, cur[0]:cur[0] + cols]
        cur[0] += cols
        v = v.bitcast(F32) if dt == F32 else v[:, 0:n]
        if len(shape) == 3:
            v = v.rearrange("p (a b) -> p a b", a=shape[1])
        if shape[0] != 128:
            v = v[0:shape[0]]
        return v

    def tb(name, shape, dt=F32):
        v = carve(arena, tail, ACOLS, shape, dt)
        if v is None:
            v = carve(aux, auxm, AUXC, shape, dt)
        return v if v is not None else sb(name, shape, dt)

    def fb(name, shape, dt=F32):
        v = carve(aux, auxf, AUXC, shape, dt)
        return v if v is not None else sb(name, shape, dt)

    ident_f = sb("ident_f", [128, 128]); ident_b = sb("ident_b", [128, 128], BF16)
    U_f = sb("U_f", [128, 128]); SL_f = sb("SL_f", [128, 128]); US_f = sb("US_f", [128, 128])
    ones_f = sb("ones_f", [128, 128]); ones_b = sb("ones_b", [1, 128], BF16)
    em.dma("c0", [(ident_f[:], c_id), (U_f[:], c_U), (SL_f[:], c_SL), (US_f[:], c_US)],
           writes=["ident_f", "U_f", "SL_f", "US_f"])
    em.op("dve", lambda: nc.vector.tensor_copy(ident_b[:], ident_f[:]), ["ident_f"], ["ident_b"])
    em.op("pool", lambda: nc.gpsimd.memset(ones_f[:], 1.0), [], ["ones_f"])
    em.op("pool", lambda: nc.gpsimd.memset(ones_b[:], 1.0), [], ["ones_b"])

    g_pf = fb("g_pf", [128, D]); g_pof = fb("g_pof", [128, D])
    prm = sb("prm", [128, 40])
    drv = sb("drv", [128, 16])
    cw = sb("cw", [128, 14, 4])
    xt = [sb(f"xt{i}", [128, D]) for i in range(3)]
    xo = sb("xo", [128, D])
    sqj = sb("sqj", [128, D], BF16)
    hb = sb("hb", [128, D], BF16)
    hT2 = [sb("hT0", [128, 8, 128], BF16), fb("hT1", [128, 8, 128], BF16)]
    hT = hT2[0]
    stat = sb("stat", [128, 16])
    etmp = [sb(f"etmp{i}", [128, 512]) for i in range(2)]
    hidT = fb("hidT", [128, NFB, 128], BF16)
    sgt2 = [fb(f"sgt{i}", [128, 512]) for i in range(2)]
    att4 = sb("att4", [128, 8])
    s8 = sb("s8", [128, 64])
    g8 = sb("g8", [128, 96])
    sK = sb("sK", [128, 4, 4]); sQ = sb("sQ", [128, 2, 4])
    sm16 = [sb(f"sm16_{i}", [128, 16]) for i in range(2)]
    dW = tb("dW", [128, 56, 128], BF16)
    g_pm = tb("g_pm", [128, D]); g_po = tb("g_po", [128, D])
    dmat = tb("dmat", [128, 1024])
    nwS = tb("nwS", [128, 512]); nwG = tb("nwG", [128, 64])
    cbrow = tb("cbrow", [1, 1024], BF16)
    xfm = tb("xfm", [128, 14, 132], BF16)
    cvs = [tb(f"cvs{i}", [128, 14, 128], BF16) for i in range(2)]
    qT = [tb(f"qT{i}", [128, 2, 128], BF16) for i in range(2)]
    kT = [tb(f"kT{i}", [128, 128], BF16) for i in range(3)]
    vaug = [tb(f"vaug{i}", [128, 2, 66], BF16) for i in range(3)]
    zs = [tb(f"zs{i}", [128, 512]) for i in range(2)]
    zsg = [tb(f"zsg{i}", [128, 256]) for i in range(2)]
    tm1 = [tb(f"tm1_{i}", [128, 768], BF16) for i in range(2)]
    tm2 = [tb(f"tm2_{i}", [128, 768], BF16) for i in range(2)]
    ex = tb("ex", [128, 1024]); pT = tb("pT", [128, 1024], BF16)
    cat = tb("cat", [128, D], BF16); catT = tb("catT", [128, 8, 128], BF16)
    lhsTd = tb("lhsTd", [128, 8, 128]); LT = tb("LT", [128, 8, 128], BF16); Gm = tb("Gm", [128, 2, 128], BF16)
    scT = tb("scT", [128, 8, 128], BF16)
    xc = tb("xc", [128, 512], BF16); xdec = tb("xdec", [128, 512], BF16)
    y1 = tb("y1", [128, 512]); y2 = tb("y2", [128, 512])
    S = tb("S", [128, 512]); Sbf = tb("Sbf", [128, 512], BF16)
    sqk = tb("sqk", [128, 512])
    Kv = tb("Kv", [128, 4, 256], BF16); Qv = tb("Qv", [128, 2, 256], BF16); vb = tb("vb", [128, 256], BF16)
    FMg = tb("FMg", [128, 8, 128], BF16)
    lDD = tb("lDD", [128, 8, 128])
    lD = lDD[:, 0:4, :]; lDT = lDD[:, 4:8, :]
    eD = tb("eD", [128, 512]); eDT = tb("eDT", [128, 512]); eDT2 = tb("eDT2", [128, 512])
    XmF = tb("XmF", [128, 4, 128]); YmF = tb("YmF", [128, 4, 128])
    TTF = tb("TTF", [128, 4, 128])
    TT = XmF.rearrange("p h c -> p (h c)").bitcast(BF16)[:, 0:512].rearrange("p (h c) -> p h c", h=4)
    QKm = tb("QKm", [128, 4, 128], BF16)
    negwT = tb("negwT", [128, 2, 128], BF16)
    vnew = tb("vnew", [128, 256], BF16)
    Sg = tb("Sg", [128, 2, 64]); Sgb = tb("Sgb", [128, 2, 64], BF16)
    osb = tb("osb", [128, 256]); sqo = tb("sqo", [128, 256])

    V = {"dve": nc.vector, "pool": nc.gpsimd}

    def tt(eng, out, a, b, op, r, w):
        em.op(eng, lambda: V[eng].tensor_tensor(out, a, b, op), r, w)

    def ts(eng, out, a, s1, s2, op0, op1, r, w):
        if s2 is None:
            em.op(eng, lambda: V[eng].tensor_scalar(out, a, s1, None, op0), r, w)
        else:
            em.op(eng, lambda: V[eng].tensor_scalar(out, a, s1, s2, op0, op1), r, w)

    def stt(eng, out, a, s, b, op0, op1, r, w):
        em.op(eng, lambda: V[eng].scalar_tensor_tensor(out, a, s, b, op0, op1), r, w)

    def cp(eng, out, a, r, w):
        if eng == "act":
            em.op("act", lambda: nc.scalar.copy(out, a), r, w)
        else:
            em.op(eng, lambda: V[eng].tensor_copy(out, a), r, w)

    def act(out, a, func, r, w, bias=None, scale=None, accum=None):
        kw = {}
        if bias is not None:
            kw["bias"] = bias
        if scale is not None:
            kw["scale"] = scale
        if accum is not None:
            kw["accum_out"] = accum
        em.op("act", lambda: nc.scalar.activation(out, a, func, **kw), r, w)

    chk_cnt = {}

    def chk(label):
        chk_cnt[label] = chk_cnt.get(label, 0) + 1
        if dbg_stop == label or dbg_stop == f"{label}#{chk_cnt[label]}":
            raise _Stop()

    def mm(out, lhsT, rhs, start=True, stop=True):
        return lambda: nc.tensor.matmul(out, lhsT, rhs, start=start, stop=stop)

    def tr(out, a):
        return lambda: nc.tensor.transpose(out, a, ident_b[:])

    def rsqrt_cols(ap, scale, r, w):
        act(ap, ap, AF.Ln, r, w, bias=EPS, scale=scale)
        act(ap, ap, AF.Exp, w, w, scale=-0.5)

    def silu(src, out, tmp, tmpk, r, w):
        act(tmp, src, AF.Exp, r, [tmpk], scale=-1.0)
        ts("pool", tmp, tmp, 1.0, None, ALU.add, None, [tmpk], [tmpk])
        em.op("dve", lambda: nc.vector.reciprocal(tmp, tmp), [tmpk], [tmpk])
        tt("dve", out, src, tmp, ALU.mult, r + [tmpk], w)

    def rmsnorm_to_hT(xtile, xkey, gain, gkey, ncols_off, hTt, hkey):
        act(sqj[:], xtile[:], AF.Square, [xkey], ["sqj", "stat0"], accum=stat[:, 0:1])
        rsqrt_cols(stat[:, 0:1], 1.0 / D, ["stat0"], ["stat0"])
        stt("dve", hb[:], xtile[:], stat[:, 0:1], gain[:], ALU.mult, ALU.mult, [xkey, "stat0", gkey], ["hb"])
        em.ops("pe", [tr(B0[:, k * 128:(k + 1) * 128], hb[:, k * 128:(k + 1) * 128]) for k in range(8)],
               ["hb", "ident_b"], ["B0"])
        cp("act", hTt[:, :, ncols_off:ncols_off + 128], B0[:].rearrange("p (k c) -> p k c", k=8), ["B0"], [hkey])

    def post_norm_residual(banks, bkeys, xin, xkey, gain, gkey):
        for i, (b, bk) in enumerate(zip(banks, bkeys)):
            act(sqj[:, 0:512], b[:, 0:512], AF.Square, [bk], ["sqj", f"stat{1 + i}"], accum=stat[:, 1 + i:2 + i])
        tt("dve", stat[:, 3:4], stat[:, 1:2], stat[:, 2:3], ALU.add, ["stat1", "stat2"], ["stat3"])
        rsqrt_cols(stat[:, 3:4], 1.0 / D, ["stat3"], ["stat3"])
        for i, (b, bk) in enumerate(zip(banks, bkeys)):
            stt("dve", xo[:, i * 512:(i + 1) * 512], b[:, 0:512], stat[:, 3:4], gain[:, i * 512:(i + 1) * 512],
                ALU.mult, ALU.mult, [bk, "stat3", gkey], [f"xo{i}"])
            tt("pool", xin[:, i * 512:(i + 1) * 512], xo[:, i * 512:(i + 1) * 512], xin[:, i * 512:(i + 1) * 512],
               ALU.add, [f"xo{i}", xkey], [xkey])

    def load_weights_M(l):
        pairs = [(w_in_sb[:, k, :], w_in_d[l, k * 128:(k + 1) * 128, :]) for k in range(8)]
        pairs += [(w_out_sb[:, k, :], w_out_d[l, k * 128:(k + 1) * 128, :]) for k in range(8)]
        em.dma("wM", pairs, writes=WM_KEYS, eng="pool")

    def load_weights_F(l):
        pairs = []
        for k in range(8):
            pairs.append((wg_sb[:, k, :], wg_d[l, k * 128:(k + 1) * 128, :]))
            pairs.append((wu_sb[:, k, :], wu_d[l, k * 128:(k + 1) * 128, :]))
        for k in range(NFB):
            pairs.append((wd_sb[:, k, :], wd_d[l, k * 128:(k + 1) * 128, :]))
        em.dma("wF", pairs, writes=WF_KEYS, eng="pool")
        em.dma("prmF", [(g_pf[:], n_pf[l, :].partition_broadcast(128)), (g_pof[:], n_pof[l, :].partition_broadcast(128))],
               writes=["g_pf", "g_pof"])

    def load_params(l):
        em.dma("prm", [(g_pm[:], n_pm[l, :].partition_broadcast(128)), (g_po[:], n_po[l, :].partition_broadcast(128)),
                       (cw[:, 0:8, :], scw_d[l]), (cw[:, 8:14, :], gcw_d[l]),
                       (prm[:, 0:8], sdtb_d[l, :].partition_broadcast(128)), (prm[:, 8:16], sal_d[l, :].partition_broadcast(128)),
                       (prm[:, 16:24], sD_d[l, :].partition_broadcast(128)), (prm[:, 24:28], sinks_d[l, :].partition_broadcast(128)),
                       (prm[:, 28:32], gdtb_d[l, :].partition_broadcast(128)), (prm[:, 32:36], gal_d[l, :].partition_broadcast(128)),
                       (nwS[:], snw_d[l, :].partition_broadcast(128)), (nwG[:], gnw_d[l, :].partition_broadcast(128)),
                       (dmat[:], c_dm)],
               writes=["g_pm", "g_po", "cw", "prm", "nwS", "nwG", "dmat"])
        em.dma("prmb", [(cbrow[:], scb_d[l:l + 1, :])], writes=["cbrow"], eng="pool")
        act(drv[:, 0:8], prm[:, 8:16], AF.Exp, ["prm"], ["drv"])
        act(drv[:, 8:12], prm[:, 32:36], AF.Exp, ["prm"], ["drv"])
        act(drv[:, 12:16], prm[:, 24:28], AF.Exp, ["prm"], ["drv"])
        ts("pool", drv[:, 0:12], drv[:, 0:12], -1.0, None, ALU.mult, None, ["drv"], ["drv"])
        tt("pool", dW[:], bc(ident_f[:], [("b", 56), "k"]),
           bc(cw[:].rearrange("p a b -> p (a b)"), ["k", ("b", 128)]), ALU.mult, ["ident_f", "cw"], ["dW"])

    FA, FB, SA, SB, GA, GB, GC = BK[1], BK[2], BK[3], BK[4], BK[5], BK[6], BK[7]

    def silu_g(src, out, tmp, tmpk, r, w):
        act(tmp, src, AF.Exp, r, [tmpk], scale=-1.0)
        yield
        act(tmp, tmp, AF.Ln, [tmpk], [tmpk], bias=1.0)
        yield
        act(tmp, tmp, AF.Exp, [tmpk], [tmpk], scale=-1.0)
        yield
        tt("dve", out, src, tmp, ALU.mult, r + [tmpk], w)
        yield

    def front(l, t, xsrc, xsrc_key):
        sl = t % 2
        s3 = t % 3
        xtile, xkey = xt[s3], f"xt{s3}"
        em.dma(f"xl{s3}", [(xtile[:], xsrc[t * 128:(t + 1) * 128, :])], reads=[f"{xsrc_key}{t}"], writes=[xkey])
        yield
        act(sqj[:], xtile[:], AF.Square, [xkey], ["sqj", "stat0"], accum=stat[:, 0:1])
        yield
        rsqrt_cols(stat[:, 0:1], 1.0 / D, ["stat0"], ["stat0"])
        yield
        stt("dve", hb[:], xtile[:], stat[:, 0:1], g_pm[:], ALU.mult, ALU.mult, [xkey, "stat0", "g_pm"], ["hb"])
        yield
        em.ops("pe", [tr(B0[:, k * 128:(k + 1) * 128], hb[:, k * 128:(k + 1) * 128]) for k in range(8)],
               ["hb", "ident_b"], ["B0"])
        cp("act", hT[:], B0[:].rearrange("p (k c) -> p k c", k=8), ["B0"], ["hT0"])
        yield
        wk = [f"win{k}" for k in range(8)]
        groups = [(0, 3), (3, 7), (7, 11), (11, 15), (15, 17)]
        if t > 0:
            cp("pool", xfm[:, :, 0:3], xfm[:, :, 128:131], ["xfm"], ["xfm"])
        for gi, (b0, b1) in enumerate(groups):
            bank, bkey = (FA, "B1") if gi % 2 == 0 else (FB, "B2")
            fns = []
            for j, b in enumerate(range(b0, b1)):
                for k in range(8):
                    fns.append(mm(bank[:, j * 128:(j + 1) * 128], w_in_sb[:, k, b * 128:(b + 1) * 128], hT[:, k, :],
                                  start=(k == 0), stop=(k == 7)))
            em.ops("pe", fns, ["hT0"] + wk, [bkey])
            yield
            if gi == 0:
                cp("act", qT[sl][:], bank[:, 0:256].rearrange("p (a c) -> p a c", a=2), [bkey], [f"qT{sl}"])
                cp("dve", kT[s3][:], bank[:, 256:384], [bkey], [f"kT{s3}"])
            else:
                nb = b1 - b0
                cp("act" if gi % 2 else "dve", xfm[:, b0 - 3:b1 - 3, 3:131],
                   bank[:, 0:nb * 128].rearrange("p (a c) -> p a c", a=nb), [bkey], ["xfm"])
            yield
        em.ops("pe", [mm(FB[:, 0:512], hT[:, k, :], w_in_sb[:, k, C_Z:C_Z + 512], start=(k == 0), stop=(k == 7))
                      for k in range(8)], ["hT0"] + wk, ["B2"])
        yield
        yield from silu_g(FB[:, 0:512], zs[sl][:], etmp[0][:], "etmp0", ["B2"], [f"zs{sl}"])
        em.ops("pe", [mm(FA[:, 0:400], hT[:, k, :], w_in_sb[:, k, C_AV:C_AV + 400], start=(k == 0), stop=(k == 7))
                      for k in range(8)], ["hT0"] + wk, ["B1"])
        yield
        cp("act", vaug[s3][:, :, 0:64], FA[:, 0:128].rearrange("p (a c) -> p a c", a=2), ["B1"], [f"vaug{s3}"])
        cp("dve", sm16[sl][:], FA[:, 384:400], ["B1"], [f"sm16_{sl}"])
        yield
        yield from silu_g(FA[:, 128:384], zsg[sl][:], etmp[1][:, 0:256], "etmp1", ["B1"], [f"zsg{sl}"])
        cgroups = [(0, 4), (4, 8), (8, 12), (12, 14)]
        for gi, (b0, b1) in enumerate(cgroups):
            bank, bkey = (FB, "B2") if gi % 2 == 0 else (FA, "B1")
            fns = []
            for j, b in enumerate(range(b0, b1)):
                o = bank[:, j * 128:(j + 1) * 128]
                for tap in range(4):
                    fns.append(mm(o, dW[:, b * 4 + tap, :], xfm[:, b, tap:tap + 128], start=(tap == 0),
                                  stop=(tap == 3 and b >= 8)))
                if b < 8:
                    fns.append(mm(o, cbrow[0:1, b * 128:(b + 1) * 128], ones_b[0:1, :], start=False, stop=True))
            em.ops("pe", fns, ["xfm", "dW", "cbrow", "ones_b"], [bkey])
            yield
            nb = b1 - b0
            yield from silu_g(bank[:, 0:nb * 128], cvs[sl][:, b0:b1, :].rearrange("p a c -> p (a c)"),
                              etmp[gi % 2][:, 0:nb * 128], f"etmp{gi % 2}", [bkey], [f"cvs{sl}"])
        em.ops("pe", [tr(B0[:, j * 128:(j + 1) * 128], cvs[sl][:, j, :]) for j in range(6)], [f"cvs{sl}", "ident_b"], ["B0"])
        cp("act", tm1[sl][:], B0[:, 0:768], ["B0"], [f"tm1_{sl}"])
        yield
        em.ops("pe", [tr(B0[:, j * 128:(j + 1) * 128], cvs[sl][:, 8 + j, :]) for j in range(6)], [f"cvs{sl}", "ident_b"], ["B0"])
        cp("act", tm2[sl][:], B0[:, 0:768], ["B0"], [f"tm2_{sl}"])
        yield

    def attn_ssd(l, t):
        sl = t % 2
        s3, p3 = t % 3, (t - 1) % 3
        kcur, kprev, vcur, vprev = kT[s3], kT[p3], vaug[s3], vaug[p3]
        kkeys = [f"kT{s3}", f"kT{p3}"]
        vkeys = [f"vaug{s3}", f"vaug{p3}"]
        heads = [(0, 0, 0), (1, 1, 0), (2, 0, 1), (3, 1, 1)]
        srcs = [1] if t == 0 else [0, 1]
        sbank = [SA, SB]
        for hf in range(2):
            fns = []
            for h, blk, half in heads:
                if half != hf:
                    continue
                for s_ in srcs:
                    kk = kprev if s_ == 0 else kcur
                    c0 = ((h % 2) * 2 + s_) * 128
                    fns.append(mm(sbank[h // 2][:, c0:c0 + 128], kk[half * 64:(half + 1) * 64, :],
                                  qT[sl][half * 64:(half + 1) * 64, blk, :]))
            em.ops("pe", fns, [f"qT{sl}"] + kkeys, ["B3", "B4"], after_self=True)
        yield
        for i in range(2):
            if t == 0:
                src_ap = sbank[i][:, 0:512].rearrange("p (a s c) -> p a s c", a=2, s=2)[:, :, 1, :]
                dst_ap = ex[:, i * 512:(i + 1) * 512].rearrange("p (a s c) -> p a s c", a=2, s=2)[:, :, 1, :]
            else:
                src_ap = sbank[i][:, 0:512]
                dst_ap = ex[:, i * 512:(i + 1) * 512]
            act(dst_ap, src_ap, AF.Exp, [f"B{3 + i}"], ["ex"], scale=0.125)
        yield
        if t == 0:
            v4 = lambda a: a.rearrange("p (a s c) -> p a s c", a=4, s=2)[:, :, 1, :]
            tt("pool", v4(pT[:]), v4(ex[:]), v4(dmat[:]), ALU.mult, ["ex", "dmat"], ["pT"])
        else:
            tt("pool", pT[:], ex[:], dmat[:], ALU.mult, ["ex", "dmat"], ["pT"])
        yield
        fns = []
        for h, blk, half in heads:
            for si, s_ in enumerate(srcs):
                vv = vprev if s_ == 0 else vcur
                c0 = (h * 2 + s_) * 128
                fns.append(mm(SA[:, h * 128:h * 128 + 65], pT[:, c0:c0 + 128], vv[:, half, 0:65],
                              start=(si == 0), stop=(si == len(srcs) - 1)))
        em.ops("pe", fns, ["pT"] + vkeys, ["B3"])
        yield
        o4 = SA[:, 0:512].rearrange("p (a c) -> p a c", a=4)
        tt("dve", att4[:, 0:4], o4[:, :, 64], drv[:, 12:16], ALU.add, ["B3", "drv"], ["att4"])
        yield
        em.op("dve", lambda: nc.vector.reciprocal(att4[:, 4:8], att4[:, 0:4]), ["att4"], ["att4"])
        yield
        tt("dve", cat[:, 0:256].rearrange("p (a c) -> p a c", a=4), o4[:, :, 0:64], bc(att4[:, 4:8], ["k", ("b", 64)]),
           ALU.mult, ["B3", "att4"], ["cat_a"])
        yield
        tm, tmk, cv, cvk, smk = tm1[sl], f"tm1_{sl}", cvs[sl], f"cvs{sl}", f"sm16_{sl}"
        xs3 = tm[:, 0:512].rearrange("p (h c) -> p h c", h=8)
        tt("dve", s8[:, 0:8], sm16[sl][:, 0:8], prm[:, 0:8], ALU.add, [smk, "prm"], ["s8a"])
        yield
        act(s8[:, 0:8], s8[:, 0:8], AF.Exp, ["s8a"], ["s8a"])
        yield
        act(s8[:, 0:8], s8[:, 0:8], AF.Ln, ["s8a"], ["s8a"], bias=1.0)
        yield
        tt("dve", s8[:, 8:16], s8[:, 0:8], drv[:, 0:8], ALU.mult, ["s8a", "drv"], ["s8b"])
        yield
        em.ops("pe", [mm(SB[:, 0:8], U_f[:], s8[:, 8:16]), mm(SB[:, 8:16], ones_f[:], s8[:, 8:16])],
               ["U_f", "ones_f", "s8b"], ["B4"])
        tt("pool", lhsTd[:], bc(SL_f[:], [("b", 8), "k"]), bc(s8[:, 8:16], ["k", ("b", 128)]), ALU.mult,
           ["SL_f", "s8b"], ["lhsTd"])
        yield
        cp("act", s8[:, 16:32], SB[:, 0:16], ["B4"], ["s8c"])
        yield
        em.ops("pe", [mm((SA if h < 4 else SB)[:, (h % 4) * 128:(h % 4 + 1) * 128], lhsTd[:, h, :], U_f[:]) for h in range(8)],
               ["lhsTd", "U_f"], ["B3", "B4"])
        act(s8[:, 32:40], s8[:, 16:24], AF.Exp, ["s8c"], ["s8d"])
        tt("dve", s8[:, 40:48], s8[:, 24:32], s8[:, 16:24], ALU.subtract, ["s8c"], ["s8e"])
        yield
        act(s8[:, 40:48], s8[:, 40:48], AF.Exp, ["s8e"], ["s8e"])
        act(s8[:, 48:56], s8[:, 24:32], AF.Exp, ["s8c"], ["s8f"])
        yield
        act(LT[:, 0:4, :].rearrange("p a c -> p (a c)"), SA[:, 0:512], AF.Exp, ["B3"], ["LT"])
        act(LT[:, 4:8, :].rearrange("p a c -> p (a c)"), SB[:, 0:512], AF.Exp, ["B4"], ["LT"])
        tt("dve", xc[:].rearrange("p (h c) -> p h c", h=8), xs3, bc(s8[:, 0:8], ["k", ("b", 64)]), ALU.mult,
           [tmk, "s8a"], ["xc"])
        tt("pool", y2[:].rearrange("p (h c) -> p h c", h=8), xs3, bc(prm[:, 16:24], ["k", ("b", 64)]), ALU.mult,
           [tmk, "prm"], ["y2"])
        yield
        em.ops("pe", [mm(SA[:, g * 128:(g + 1) * 128], cv[:, 4 + g, :], cv[:, 6 + g, :]) for g in range(2)],
               [cvk], ["B3"])
        yield
        tt("dve", Gm[:], SA[:, 0:256].rearrange("p (g c) -> p g c", g=2), bc(U_f[:], [("b", 2), "k"]), ALU.mult,
           ["B3", "U_f"], ["Gm"])
        yield
        tt("pool", scT[:].rearrange("p (g a) c -> p g a c", g=2), LT[:].rearrange("p (g a) c -> p g a c", g=2),
           bc(Gm[:], ["k", ("b", 4), "k"]), ALU.mult, ["LT", "Gm"], ["scT"])
        tt("pool", xdec[:].rearrange("p (h c) -> p h c", h=8), xc[:].rearrange("p (h c) -> p h c", h=8),
           bc(s8[:, 40:48], ["k", ("b", 64)]), ALU.mult, ["xc", "s8e"], ["xdec"])
        yield
        em.ops("pe", [mm(SB[:, h * 64:(h + 1) * 64], scT[:, h, :], xc[:, h * 64:(h + 1) * 64]) for h in range(8)],
               ["scT", "xc"], ["B4"])
        if t > 0:
            em.ops("pe", [mm(SA[:, g * 256:(g + 1) * 256], cv[:, 6 + g, :], Sbf[:, g * 256:(g + 1) * 256]) for g in range(2)],
                   [cvk, "Sbf"], ["B3"])
        yield
        if t > 0:
            tt("dve", y1[:].rearrange("p (h c) -> p h c", h=8), SA[:, 0:512].rearrange("p (h c) -> p h c", h=8),
               bc(s8[:, 32:40], ["k", ("b", 64)]), ALU.mult, ["B3", "s8d"], ["y1"])
            yield
            tt("dve", y1[:], y1[:], SB[:, 0:512], ALU.add, ["y1", "B4"], ["y1"])
            yield
            tt("pool", y1[:], y1[:], y2[:], ALU.add, ["y1", "y2"], ["y1"])
        else:
            tt("dve", y1[:], y2[:], SB[:, 0:512], ALU.add, ["y2", "B4"], ["y1"])
        yield
        em.ops("pe", [mm(SB[:, g * 256:(g + 1) * 256], tm[:, 512 + g * 128:512 + (g + 1) * 128], xdec[:, g * 256:(g + 1) * 256])
                      for g in range(2)], [tmk, "xdec"], ["B4"])
        tt("dve", y1[:], y1[:], zs[sl][:], ALU.mult, ["y1", f"zs{sl}"], ["y1"])
        yield
        for g in range(2):
            act(sqj[:, g * 256:(g + 1) * 256], y1[:, g * 256:(g + 1) * 256], AF.Square, ["y1"], ["sqj", "s8g"],
                accum=s8[:, 56 + g:57 + g])
        if t > 0:
            tt("pool", S[:].rearrange("p (h c) -> p h c", h=8), S[:].rearrange("p (h c) -> p h c", h=8),
               bc(s8[:, 48:56], ["k", ("b", 64)]), ALU.mult, ["S", "s8f"], ["S"])
        yield
        rsqrt_cols(s8[:, 56:58], 1.0 / 256, ["s8g"], ["s8g"])
        if t > 0:
            tt("dve", S[:], S[:], SB[:, 0:512], ALU.add, ["S", "B4"], ["S"])
        else:
            cp("dve", S[:], SB[:, 0:512], ["B4"], ["S"])
        yield
        cp("pool", Sbf[:], S[:], ["S"], ["Sbf"])
        for g in range(2):
            stt("dve", cat[:, 256 + g * 256:512 + g * 256], y1[:, g * 256:(g + 1) * 256], s8[:, 56 + g:57 + g],
                nwS[:, g * 256:(g + 1) * 256], ALU.mult, ALU.mult, ["y1", "s8g", "nwS"], ["cat_s"])
        yield

    def gdn(l, t):
        sl = t % 2
        tm, tmk, smk, zg, zgk = tm2[sl], f"tm2_{sl}", f"sm16_{sl}", zsg[sl], f"zsg{sl}"
        sm = sm16[sl]
        q3 = tm[:, 0:256].rearrange("p (h c) -> p h c", h=4)
        k3 = tm[:, 256:512].rearrange("p (h c) -> p h c", h=4)
        v3 = tm[:, 512:768].rearrange("p (h c) -> p h c", h=4)
        tt("dve", sqk[:], tm[:, 0:512], tm[:, 0:512], ALU.mult, [tmk], ["sqk"])
        act(g8[:, 8:12], sm[:, 8:12], AF.Exp, [smk], ["g8b"], scale=-1.0)
        tt("dve", g8[:, 12:16], sm[:, 12:16], prm[:, 28:32], ALU.add, [smk, "prm"], ["g8c"])
        yield
        em.op("dve", lambda: nc.vector.tensor_reduce(g8[:, 0:8], sqk[:].rearrange("p (h c) -> p h c", h=8), AX.X, ALU.add),
              ["sqk"], ["g8a"])
        act(g8[:, 12:16], g8[:, 12:16], AF.Exp, ["g8c"], ["g8c"])
        yield
        act(g8[:, 12:16], g8[:, 12:16], AF.Ln, ["g8c"], ["g8c"], bias=1.0)
        ts("dve", g8[:, 8:12], g8[:, 8:12], 1.0, None, ALU.add, None, ["g8b"], ["g8b"])
        yield
        rsqrt_cols(g8[:, 0:8], 1.0, ["g8a"], ["g8a"])
        em.op("dve", lambda: nc.vector.reciprocal(g8[:, 8:12], g8[:, 8:12]), ["g8b"], ["g8b"])
        yield
        tt("dve", g8[:, 12:16], g8[:, 12:16], drv[:, 8:12], ALU.mult, ["g8c", "drv"], ["g8c"])
        yield
        em.ops("pe", [mm(GA[:, 0:4], U_f[:], g8[:, 12:16]), mm(GA[:, 4:8], ones_f[:], g8[:, 12:16])],
               ["U_f", "ones_f", "g8c"], ["B5"])
        tt("pool", lD, bc(U_f[:], [("b", 4), "k"]), bc(g8[:, 12:16], ["k", ("b", 128)]), ALU.mult, ["U_f", "g8c"], ["lDD"])
        tt("pool", lDT, bc(SL_f[:], [("b", 4), "k"]), bc(g8[:, 12:16], ["k", ("b", 128)]), ALU.mult, ["SL_f", "g8c"], ["lDD"])
        yield
        cp("act", g8[:, 16:24], GA[:, 0:8], ["B5"], ["g8d"])
        em.ops("pe", [mm(GB[:, h * 128:(h + 1) * 128], lD[:, h, :], SL_f[:]) for h in range(4)]
               + [mm(GC[:, h * 128:(h + 1) * 128], lDT[:, h, :], U_f[:]) for h in range(4)],
               ["lDD", "SL_f", "U_f"], ["B6", "B7"])
        yield
        act(g8[:, 24:28], g8[:, 16:20], AF.Exp, ["g8d"], ["g8e"])
        tt("dve", g8[:, 28:32], g8[:, 20:24], g8[:, 16:20], ALU.subtract, ["g8d"], ["g8f"])
        yield
        act(g8[:, 28:32], g8[:, 28:32], AF.Exp, ["g8f"], ["g8f"])
        act(g8[:, 32:36], g8[:, 20:24], AF.Exp, ["g8d"], ["g8g"])
        yield
        act(eD[:], GB[:, 0:512], AF.Exp, ["B6"], ["eD"])
        act(eDT[:], GC[:, 0:512], AF.Exp, ["B7"], ["eDT"])
        cp("dve", sK[:, 0, :], g8[:, 4:8], ["g8a"], ["sK"])
        tt("dve", sK[:, 1, :], g8[:, 4:8], g8[:, 8:12], ALU.mult, ["g8a", "g8b"], ["sK"])
        yield
        tt("dve", sK[:, 2, :], sK[:, 1, :], g8[:, 24:28], ALU.mult, ["sK", "g8e"], ["sK"])
        tt("dve", sK[:, 3, :], g8[:, 4:8], g8[:, 28:32], ALU.mult, ["g8a", "g8f", "sK"], ["sK"])
        ts("dve", sQ[:, 0, :], g8[:, 0:4], 0.125, None, ALU.mult, None, ["g8a"], ["sQ"])
        yield
        tt("dve", sQ[:, 1, :], sQ[:, 0, :], g8[:, 24:28], ALU.mult, ["sQ", "g8e"], ["sQ"])
        tt("pool", Kv[:].rearrange("p v (h c) -> p v h c", h=4), bc(k3, [("b", 4), "k", "k"]), bc(sK[:], ["k", "k", ("b", 64)]),
           ALU.mult, [tmk, "sK"], ["Kv"])
        yield
        tt("dve", Qv[:].rearrange("p v (h c) -> p v h c", h=4), bc(q3, [("b", 2), "k", "k"]), bc(sQ[:], ["k", "k", ("b", 64)]),
           ALU.mult, [tmk, "sQ"], ["Qv"])
        tt("pool", vb[:].rearrange("p (h c) -> p h c", h=4), v3, bc(g8[:, 8:12], ["k", ("b", 64)]), ALU.mult,
           [tmk, "g8b"], ["vb"])
        m4 = lambda a: a.rearrange("p (h c) -> p h c", h=4)
        tt("pool", m4(eD[:]), m4(eD[:]), bc(SL_f[:], [("b", 4), "k"]), ALU.mult, ["eD", "SL_f"], ["eD"])
        yield
        tt("pool", m4(eDT2[:]), m4(eDT[:]), bc(U_f[:], [("b", 4), "k"]), ALU.mult, ["eDT", "U_f"], ["eDT2"])
        tt("pool", m4(eDT[:]), m4(eDT[:]), bc(US_f[:], [("b", 4), "k"]), ALU.mult, ["eDT", "US_f", "eDT2"], ["eDT"])
        yield
        srcT = [Kv[:, 0, 0:128], Kv[:, 0, 128:256], Kv[:, 1, 0:128], Kv[:, 1, 128:256],
                Qv[:, 0, 0:128], Qv[:, 0, 128:256], Qv[:, 1, 0:128], Qv[:, 1, 128:256]]
        em.ops("pe", [tr(B0[:, j * 128:(j + 1) * 128], srcT[j]) for j in range(8)], ["Kv", "Qv", "ident_b"], ["B0"])
        cp("act", FMg[:], B0[:].rearrange("p (k c) -> p k c", k=8), ["B0"], ["FMg"])
        yield

        def fm(var, h):
            pair, half = h // 2, h % 2
            return FMg[half * 64:(half + 1) * 64, var * 2 + pair, :]

        for half in range(2):
            fns = []
            for bank, (va, vb_) in ((GA, (1, 0)), (GB, (0, 1)), (GC, (0, 2))):
                for pair in range(2):
                    h = pair * 2 + half
                    fns.append(mm(bank[:, h * 128:(h + 1) * 128], fm(va, h), fm(vb_, h)))
            em.ops("pe", fns, ["FMg"], ["B5", "B6", "B7"], after_self=True)
        yield
        f4 = lambda a: a.rearrange("p h c -> p (h c)")
        tt("dve", f4(XmF[:]), GA[:, 0:512], eD[:], ALU.mult, ["B5", "eD"], ["XmF"])
        yield
        tt("dve", f4(YmF[:]), GB[:, 0:512], eDT[:], ALU.mult, ["B6", "eDT"], ["YmF"])
        yield
        tt("dve", f4(QKm[:]), GC[:, 0:512], eDT2[:], ALU.mult, ["B7", "eDT2"], ["QKm"])
        tt("pool", TTF[:], bc(ident_f[:], [("b", 4), "k"]), YmF[:], ALU.subtract, ["ident_f", "YmF"], ["TTF"])
        yield
        for lev in range(6):
            last = lev == 5
            em.ops("pe", [mm(GA[:, h * 128:(h + 1) * 128], YmF[:, h, :], XmF[:, h, :]) for h in range(4)],
                   ["XmF", "YmF"], ["B5"])
            if not last:
                em.ops("pe", [mm(GB[:, h * 128:(h + 1) * 128], XmF[:, h, :], YmF[:, h, :]) for h in range(4)],
                       ["XmF", "YmF"], ["B6"])
            yield
            cp("act", f4(XmF[:]), GA[:, 0:512], ["B5"], ["XmF"])
            if not last:
                cp("dve", f4(YmF[:]), GB[:, 0:512], ["B6"], ["YmF"])
            yield
            em.ops("pe", [mm(GC[:, h * 128:(h + 1) * 128], XmF[:, h, :], TTF[:, h, :]) for h in range(4)],
                   ["XmF", "TTF"], ["B7"])
            yield
            tt("dve", f4(TTF[:]), f4(TTF[:]), GC[:, 0:512], ALU.add, ["TTF", "B7"], ["TTF"])
            yield
        cp("pool", TT, TTF[:], ["TTF", "XmF"], ["TT", "XmF"])
        yield
        em.ops("pe", [mm(GA[:, h * 128:(h + 1) * 128], Kv[:, 2, (h // 2) * 128:(h // 2 + 1) * 128], TT[:, h, :]) for h in range(4)],
               ["Kv", "TT"], ["B5"])
        yield
        for half in range(2):
            src_ap = GA[half * 64:(half + 1) * 64, 0:512].rearrange("p (pr hh c) -> p pr hh c", pr=2, hh=2)[:, :, half, :]
            em.op("act", lambda src_ap=src_ap, half=half: nc.scalar.mul(negwT[half * 64:(half + 1) * 64, :, :], src_ap, -1.0),
                  ["B5"], ["negwT"])
        yield
        for h in range(4):
            pair, half = h // 2, h % 2
            o = GB[:, h * 64:(h + 1) * 64]
            fns = [mm(o, TT[:, h, :], vb[:, h * 64:(h + 1) * 64], start=True, stop=(t == 0))]
            if t > 0:
                fns.append(mm(o, negwT[half * 64:(half + 1) * 64, pair, :], Sgb[half * 64:(half + 1) * 64, pair, :],
                              start=False, stop=True))
            em.ops("pe", fns, ["TT", "vb", "negwT", "Sgb"], ["B6"], after_self=True)
        yield
        cp("act", vnew[:], GB[:, 0:256], ["B6"], ["vnew"])
        yield
        for h in range(4):
            pair, half = h // 2, h % 2
            o = GC[:, h * 64:(h + 1) * 64]
            fns = [mm(o, QKm[:, h, :], vnew[:, h * 64:(h + 1) * 64], start=True, stop=(t == 0))]
            if t > 0:
                fns.append(mm(o, fm(3, h), Sgb[half * 64:(half + 1) * 64, pair, :], start=False, stop=True))
            em.ops("pe", fns, ["QKm", "vnew", "FMg", "Sgb"], ["B7"], after_self=True)
        em.ops("pe", [mm(GA[:, pr * 128:(pr + 1) * 128], Kv[:, 3, pr * 128:(pr + 1) * 128], vnew[:, pr * 128:(pr + 1) * 128])
                      for pr in range(2)], ["Kv", "vnew", "negwT"], ["B5"], after_self=True)
        yield
        cp("act", osb[:], GC[:, 0:256], ["B7"], ["osb"])
        for h in range(4):
            pair, half = h // 2, h % 2
            rows = slice(half * 64, (half + 1) * 64)
            blk = GA[rows, pair * 128 + half * 64:pair * 128 + (half + 1) * 64]
            if t > 0:
                stt("dve", Sg[rows, pair, :], Sg[rows, pair, :], g8[rows, 32 + h:33 + h], blk, ALU.mult, ALU.add,
                    ["Sg", "g8g", "B5", "Sgb"], ["Sg"])
            else:
                cp("dve", Sg[rows, pair, :], blk, ["B5"], ["Sg"])
        yield
        cp("pool", Sgb[:], Sg[:], ["Sg"], ["Sgb"])
        tt("pool", sqo[:], osb[:], osb[:], ALU.mult, ["osb"], ["sqo"])
        yield
        em.op("dve", lambda: nc.vector.tensor_reduce(g8[:, 36:40], sqo[:].rearrange("p (h c) -> p h c", h=4), AX.X, ALU.add),
              ["sqo"], ["g8h"])
        tt("pool", zg[:].rearrange("p (h c) -> p h c", h=4), zg[:].rearrange("p (h c) -> p h c", h=4),
           bc(nwG[:], [("b", 4), "k"]), ALU.mult, [zgk, "nwG"], [zgk])
        yield
        rsqrt_cols(g8[:, 36:40], 1.0 / 64, ["g8h"], ["g8h"])
        yield
        for h in range(4):
            stt("dve", cat[:, 768 + h * 64:768 + (h + 1) * 64], osb[:, h * 64:(h + 1) * 64], g8[:, 36 + h:37 + h],
                zg[:, h * 64:(h + 1) * 64], ALU.mult, ALU.mult, ["osb", "g8h", zgk], ["cat_g"])
        yield

    def back(l, t, xdst, xdst_key):
        s3 = t % 3
        if dbg_d is not None and l == 0 and t == 1:
            cp("dve", ex[:], cat[:], ["cat_a", "cat_s", "cat_g"], ["ex"])
            em.dma("dbg", [(dbg_d, ex[:])], reads=["ex"], writes=["dbgout"])
        em.ops("pe", [tr(B0[:, k * 128:(k + 1) * 128], cat[:, k * 128:(k + 1) * 128]) for k in range(8)],
               ["cat_a", "cat_s", "cat_g", "ident_b"], ["B0"])
        cp("act", catT[:], B0[:].rearrange("p (k c) -> p k c", k=8), ["B0"], ["catT"])
        yield
        for nb in range(2):
            em.ops("pe", [mm(BK[3 + nb][:, 0:512], catT[:, k, :], w_out_sb[:, k, nb * 512:(nb + 1) * 512], start=(k == 0), stop=(k == 7))
                          for k in range(8)], ["catT"] + [f"wout{k}" for k in range(8)], [f"B{3 + nb}"])
        yield
        for i, (b, bk) in enumerate(((SA, "B3"), (SB, "B4"))):
            act(sqj[:, 0:512], b[:, 0:512], AF.Square, [bk], ["sqj", f"stat{1 + i}"], accum=stat[:, 1 + i:2 + i])
        yield
        tt("dve", stat[:, 3:4], stat[:, 1:2], stat[:, 2:3], ALU.add, ["stat1", "stat2"], ["stat3"])
        yield
        rsqrt_cols(stat[:, 3:4], 1.0 / D, ["stat3"], ["stat3"])
        yield
        for i, (b, bk) in enumerate(((SA, "B3"), (SB, "B4"))):
            stt("dve", xo[:, i * 512:(i + 1) * 512], b[:, 0:512], stat[:, 3:4], g_po[:, i * 512:(i + 1) * 512],
                ALU.mult, ALU.mult, [bk, "stat3", "g_po"], [f"xo{i}"])
            yield
            tt("pool", xt[s3][:, i * 512:(i + 1) * 512], xo[:, i * 512:(i + 1) * 512], xt[s3][:, i * 512:(i + 1) * 512],
               ALU.add, [f"xo{i}", f"xt{s3}"], [f"xt{s3}"])
            yield
        em.dma(f"xs{s3}", [(xdst[t * 128:(t + 1) * 128, :], xt[s3][:])], reads=[f"xt{s3}"], writes=[f"{xdst_key}{t}"])
        yield

    def thread2(l, t, xdst, xdst_key):
        if t > 0:
            yield from back(l, t - 1, xdst, xdst_key)
        yield from attn_ssd(l, t)

    def run_threads(gens):
        gens = list(gens)
        while gens:
            for g_ in list(gens):
                try:
                    next(g_)
                except StopIteration:
                    gens.remove(g_)

    def ffn_front(t, xsrc, xsrc_key):
        sl, s3 = t % 2, t % 3
        em.dma(f"xl{s3}", [(xt[s3][:], xsrc[t * 128:(t + 1) * 128, :])], reads=[f"{xsrc_key}{t}"], writes=[f"xt{s3}"])
        rmsnorm_to_hT(xt[s3], f"xt{s3}", g_pf, "g_pf", 0, hT2[sl], f"hT{sl}")

    def ffn_tile(l, t, xsrc, xsrc_key, xdst, xdst_key):
        sl = t % 2
        hTt, hkey = hT2[sl], f"hT{sl}"
        nun = (NFB + 3) // 4
        for u in range(nun):
            f0, f1 = u * 4, min(NFB, u * 4 + 4)
            nb = f1 - f0
            w = nb * 128
            gbank, ubank = (BK[1], BK[2]) if u % 2 == 0 else (BK[3], BK[4])
            gk, uk = ("B1", "B2") if u % 2 == 0 else ("B3", "B4")
            fns = []
            for j, fb in enumerate(range(f0, f1)):
                for k in range(8):
                    fns.append(mm(gbank[:, j * 128:(j + 1) * 128], wg_sb[:, k, fb * 128:(fb + 1) * 128], hTt[:, k, :],
                                  start=(k == 0), stop=(k == 7)))
                for k in range(8):
                    fns.append(mm(ubank[:, j * 128:(j + 1) * 128], wu_sb[:, k, fb * 128:(fb + 1) * 128], hTt[:, k, :],
                                  start=(k == 0), stop=(k == 7)))
            em.ops("pe", fns, [hkey] + [f"wg{k}" for k in range(8)] + [f"wu{k}" for k in range(8)], [gk, uk])
            e, ek = etmp[u % 2][:, 0:w], f"etmp{u % 2}"
            sg, sk = sgt2[u % 2][:, 0:w], f"sgt{u % 2}"
            act(e, gbank[:, 0:w], AF.Exp, [gk], [ek], scale=-1.0)
            act(e, e, AF.Ln, [ek], [ek], bias=1.0)
            act(e, e, AF.Exp, [ek], [ek], scale=-1.0)
            tt("dve", sg, gbank[:, 0:w], e, ALU.mult, [gk, ek], [sk])
            tt("dve", hidT[:, f0:f1, :].rearrange("p a c -> p (a c)"), sg, ubank[:, 0:w], ALU.mult, [sk, uk], ["hidT"])
            if u == 1 and t + 1 < NT:
                ffn_front(t + 1, xsrc, xsrc_key)
        for nb_ in range(2):
            em.ops("pe", [mm(BK[6 + nb_][:, 0:512], hidT[:, fc, :], wd_sb[:, fc, nb_ * 512:(nb_ + 1) * 512],
                             start=(fc == 0), stop=(fc == NFB - 1)) for fc in range(NFB)],
                   ["hidT"] + [f"wd{k}" for k in range(NFB)], [f"B{6 + nb_}"])
        s3 = t % 3
        post_norm_residual([BK[6], BK[7]], ["B6", "B7"], xt[s3], f"xt{s3}", g_pof, "g_pof")
        em.dma(f"xs{s3}", [(xdst[t * 128:(t + 1) * 128, :], xt[s3][:])], reads=[f"xt{s3}"], writes=[f"{xdst_key}{t}"])

    try:
      for l in range(L):
        if l > 0:
            em.barrier()
        load_weights_M(l)
        load_params(l)
        for s_ in range(3):
            em.op("pool", lambda s_=s_: nc.gpsimd.memset(vaug[s_][:, :, 64:65], 1.0), [], [f"vaug{s_}"])
        em.op("pool", lambda: nc.gpsimd.memset(xfm[:, :, 0:3], 0.0), [], ["xfm"])
        xsrc, xkey = (x_d, "xin") if l == 0 else (xb_d, f"xb{l - 1}_")
        run_threads([front(l, 0, xsrc, xkey)])
        for t in range(NT):
            th = [gdn(l, t), thread2(l, t, xa_d, f"xa{l}_")]
            if t + 1 < NT:
                th.append(front(l, t + 1, xsrc, xkey))
            run_threads(th)
            chk("mix")
        run_threads([back(l, NT - 1, xa_d, f"xa{l}_")])
        em.barrier()
        load_weights_F(l)
        last = l == L - 1
        ffn_front(0, xa_d, f"xa{l}_")
        for t in range(NT):
            ffn_tile(l, t, xa_d, f"xa{l}_", out_d if last else xb_d, "out" if last else f"xb{l}_")
            chk("ffn")
    except _Stop:
        em.barrier()
    em.finish([f"out{t}" for t in range(NT)])
    em.emit()
    return nc, st, em


def _consts():
    k = np.arange(128)
    U = (k[:, None] <= k[None, :]).astype(np.float32)
    SLm = (k[:, None] > k[None, :]).astype(np.float32)
    US = (k[:, None] < k[None, :]).astype(np.float32)
    ident = np.eye(128, dtype=np.float32)
    slopes = 2.0 ** (-8.0 * np.arange(1, 5, dtype=np.float64) / 4)
    dm = np.zeros((128, 4, 2, 128), np.float64)
    for s_ in range(2):
        tk = k[:, None] + (s_ - 1) * 128
        tq = k[None, :]
        qc = tq // 64
        kc_ = np.floor_divide(tk, 64)
        valid = (kc_ >= qc - 2) & (kc_ <= qc)
        for h in range(4):
            dm[:, h, s_, :] = np.where(valid, np.exp(-slopes[h] * np.abs(tq - tk)), 0.0)
    return ident, U, SLm, US, dm.reshape(128, 1024).astype(np.float32)


def make_in_maps(inputs, NT, n_cores):
    L = inputs["w_in"].shape[0]
    perm = _win_perm()
    ident, U, SLm, US, dm = _consts()
    f = lambda a: np.ascontiguousarray(np.asarray(a, dtype=np.float32))
    shared = {
        "pre_mix_norm": f(inputs["pre_mix_norm"]), "post_mix_norm": f(inputs["post_mix_norm"]),
        "pre_ffn_norm": f(inputs["pre_ffn_norm"]), "post_ffn_norm": f(inputs["post_ffn_norm"]),
        "w_in": f(np.asarray(inputs["w_in"])[:, :, perm]), "w_out": f(inputs["w_out"]),
        "attn_sinks": f(inputs["attn_sinks"]),
        "ssd_conv_w": f(np.asarray(inputs["ssd_conv_w"]).reshape(L, 4, 8, 128).transpose(0, 3, 2, 1)),
        "ssd_conv_b": f(inputs["ssd_conv_b"]), "ssd_dt_bias": f(inputs["ssd_dt_bias"]),
        "ssd_A_log": f(inputs["ssd_A_log"]), "ssd_D": f(inputs["ssd_D"]), "ssd_norm_w": f(inputs["ssd_norm_w"]),
        "gdn_conv_w": f(np.asarray(inputs["gdn_conv_w"]).reshape(L, 4, 6, 128).transpose(0, 3, 2, 1)),
        "gdn_dt_bias": f(inputs["gdn_dt_bias"]), "gdn_A_log": f(inputs["gdn_A_log"]), "gdn_norm_w": f(inputs["gdn_norm_w"]),
        "ffn_w_gate": f(inputs["ffn_w_gate"]), "ffn_w_up": f(inputs["ffn_w_up"]), "ffn_w_down": f(inputs["ffn_w_down"]),
        "c_ident": ident, "c_U": U, "c_SL": SLm, "c_US": US, "c_dmat": dm,
    }
    x = np.asarray(inputs["x"], dtype=np.float32)
    nb = x.shape[0]
    maps = []
    if n_cores == 8 and nb == 4:
        zero = {k: (v if k.startswith("c_") else np.zeros_like(v)) for k, v in shared.items()}
        for c in range(n_cores):
            if c in ACTIVE_CORES:
                m = dict(shared)
                m["x"] = np.ascontiguousarray(x[ACTIVE_CORES.index(c), :NT * 128])
            else:
                m = dict(zero)
                m["x"] = np.zeros((NT * 128, x.shape[2]), np.float32)
            maps.append(m)
        return maps
    for c in range(n_cores):
        m = dict(shared)
        m["x"] = np.ascontiguousarray(x[c % nb, :NT * 128])
        maps.append(m)
    return maps


ACTIVE_CORES = [0, 1, 4, 5]


def kernel(**inputs):
    x = np.asarray(inputs["x"])
    Bn, T, _ = x.shape
    NT = T // 128
    L = np.asarray(inputs["w_in"]).shape[0]
    nc, st, em = build_program(NT, L)
    maps = make_in_maps(inputs, NT, 8)
    res = run_bass_kernel_spmd(nc, maps, core_ids=list(range(8)))
    src = ACTIVE_CORES if Bn == 4 else list(range(Bn))
    out = np.stack([np.asarray(res.results[src[b]]["out"], dtype=np.float32) for b in range(Bn)], axis=0)
    return out
```
